# Optimizing a Trainium2 kernel written in Bass

```python
import jax, jax.numpy as jnp
from jax import lax
import numpy as np


D_MODEL = 1024
BATCH = 32
SEQ = 2048
DEPTH = 4

MEM_LEN = 256
HEAD_DIM = 64
ROPE_THETA = 10000.0
NORM_EPS = 1e-6
NEG_INF = -1e30
D_FF = 2816
CONV_W = 512
CONV_K = 31
POOL_W = 512
POOL_GROUPS = 4
POOL_WINDOWS = (2, 4, 8, 16)
NSA_HEADS = 8
NSA_KV = 2
CMP_BLOCK = 32
CMP_STRIDE = 16
CMP_HIDDEN = 256
SLC_BLOCK = 64
N_SEL = 16
FORCE_BONUS = 1e4
NSA_WINDOW = 512
NSA_QCHUNK = 16
SWA_HEADS = 8
SWA_KV = 2
SWA_WINDOW = 128
Q_BLOCK = 128
X_HEADS = 4
X_HEAD_DIM = D_MODEL // X_HEADS
N_BRANCH = 4
BRANCH_W = 512
IN_SIZES = (2 * CONV_W, POOL_W, NSA_HEADS * HEAD_DIM, 6 * NSA_KV * HEAD_DIM, 3 * NSA_HEADS,
            SWA_HEADS * HEAD_DIM, 2 * SWA_KV * HEAD_DIM, N_BRANCH * D_MODEL)
D_IN = sum(IN_SIZES)

kernel_name = 'hybrid_gated_mixers_trunk'


def rms_norm(x, g):
    xf = x.astype(jnp.float32)
    y = xf * lax.rsqrt(jnp.mean(xf * xf, -1, keepdims=True) + NORM_EPS)
    return (y * g.astype(jnp.float32)).astype(x.dtype)


def layer_norm(x, g, b):
    xf = x.astype(jnp.float32)
    mu = jnp.mean(xf, -1, keepdims=True)
    var = jnp.mean(jnp.square(xf - mu), -1, keepdims=True)
    y = (xf - mu) * lax.rsqrt(var + NORM_EPS)
    return (y * g.astype(jnp.float32) + b.astype(jnp.float32)).astype(x.dtype)


def rope_tables(positions):
    inv = ROPE_THETA ** (-jnp.arange(0, HEAD_DIM, 2, dtype=jnp.float32) / HEAD_DIM)
    ang = positions.astype(jnp.float32)[..., None] * inv
    return jnp.cos(ang)[:, :, None, :], jnp.sin(ang)[:, :, None, :]


def apply_rope(x, cos, sin):
    half = x.shape[-1] // 2
    xf = x.astype(jnp.float32)
    x1, x2 = xf[..., :half], xf[..., half:]
    return jnp.concatenate([x1 * cos - x2 * sin, x2 * cos + x1 * sin], -1).astype(x.dtype)


def masked_softmax(s, mask):
    sm = jnp.where(mask, s, NEG_INF)
    e = jnp.exp(sm - jnp.max(sm, -1, keepdims=True)) * mask
    return e / jnp.maximum(jnp.sum(e, -1, keepdims=True), 1e-30)


def swiglu(h, w_in, w_out):
    a, b = jnp.split(h @ w_in, 2, axis=-1)
    return (jax.nn.silu(a) * b) @ w_out


def conv_module(u, conv_w, conv_b, ln_g, ln_b):
    a, g = jnp.split(u, 2, axis=-1)
    v = a * jax.nn.sigmoid(g)
    y = lax.conv_general_dilated(v, conv_w[:, None, :], window_strides=(1,),
                                 padding=[(CONV_K - 1, 0)],
                                 dimension_numbers=('NWC', 'WIO', 'NWC'),
                                 feature_group_count=CONV_W) + conv_b
    return jax.nn.silu(layer_norm(y, ln_g, ln_b))


def pool_mixer(v, w_pool, scale):
    B, S, C = v.shape
    cg_w = C // POOL_GROUPS
    vf = v.astype(jnp.float32)
    c = jnp.concatenate([jnp.zeros((B, 1, C), jnp.float32), jnp.cumsum(vf, axis=1)], axis=1)
    t = jnp.arange(S)
    outs = []
    for g, w in enumerate(POOL_WINDOWS):
        cg = c[..., g * cg_w:(g + 1) * cg_w]
        lag = jnp.pad(cg, ((0, 0), (w - 1, 0), (0, 0)))[:, :S]
        cnt = jnp.minimum(t + 1, w).astype(jnp.float32)[None, :, None]
        outs.append((cg[:, 1:] - lag) / cnt - vf[..., g * cg_w:(g + 1) * cg_w])
    d = jnp.stack(outs, axis=2).astype(v.dtype)
    y = jnp.einsum('bsgc,gcd->bsgd', d, w_pool).reshape(B, S, C)
    return y * scale


def banded_attention(q, k, v, window, sinks=None):
    B, S, H, dh = q.shape
    G = k.shape[2]
    R = H // G
    n_blk = S // Q_BLOCK
    L = Q_BLOCK + window
    pad = ((0, 0), (window, 0), (0, 0), (0, 0))
    kp = jnp.pad(k, pad)
    vp = jnp.pad(v, pad)
    qb = jnp.moveaxis(q.reshape(B, n_blk, Q_BLOCK, G, R, dh), 1, 0)
    scale = dh ** -0.5

    def block(args):
        i, qi = args
        start = i * Q_BLOCK
        ki = lax.dynamic_slice_in_dim(kp, start, L, axis=1)
        vi = lax.dynamic_slice_in_dim(vp, start, L, axis=1)
        qpos = start + jnp.arange(Q_BLOCK)
        kpos = start - window + jnp.arange(L)
        diff = qpos[:, None] - kpos[None, :]
        mask = (diff >= 0) & (diff < window) & (kpos[None, :] >= 0)
        s = jnp.einsum('bqgrd,bkgd->bgrqk', qi, ki).astype(jnp.float32) * scale
        s = jnp.where(mask, s, NEG_INF)
        if sinks is None:
            p = jax.nn.softmax(s, axis=-1)
        else:
            sk = sinks.astype(jnp.float32).reshape(G, R)[None, :, :, None, None]
            m = jnp.maximum(jnp.max(s, -1, keepdims=True), sk)
            e = jnp.exp(s - m)
            p = e / (jnp.sum(e, -1, keepdims=True) + jnp.exp(sk - m))
        return jnp.einsum('bgrqk,bkgd->bqgrd', p.astype(v.dtype), vi)

    o = lax.map(block, (jnp.arange(n_blk), qb))
    return jnp.moveaxis(o, 0, 1).reshape(B, S, H, dh)


def nsa_attention(q, k_c, v_c, k_s, v_s, k_w, v_w, gate_logits,
                  k_pos, k_w1, k_b1, k_w2, k_b2, v_pos, v_w1, v_b1, v_w2, v_b2):
    B, S, H, dh = q.shape
    G = k_c.shape[2]
    R = H // G
    scale = dh ** -0.5
    n_c = (S - CMP_BLOCK) // CMP_STRIDE + 1
    cmp_start = np.arange(n_c) * CMP_STRIDE
    cmp_idx = cmp_start[:, None] + np.arange(CMP_BLOCK)[None, :]
    cmp_end = jnp.asarray(cmp_start + CMP_BLOCK - 1, jnp.int32)

    def compress(t, pos, w1, b1, w2, b2):
        blk = t[:, cmp_idx] + pos[None, None, :, None, :]
        blk = jnp.moveaxis(blk, 3, 2).reshape(B, n_c, G, CMP_BLOCK * dh)
        return jax.nn.gelu(blk @ w1 + b1) @ w2 + b2

    k_cmp = compress(k_c, k_pos, k_w1, k_b1, k_w2, k_b2)
    v_cmp = compress(v_c, v_pos, v_w1, v_b1, v_w2, v_b2)
    n_s = S // SLC_BLOCK
    sel_start = np.arange(n_s) * SLC_BLOCK
    overlap = jnp.asarray(((cmp_start[:, None] <= sel_start[None, :] + SLC_BLOCK - 1) &
                           (cmp_start[:, None] + CMP_BLOCK - 1 >= sel_start[None, :])).astype(np.float32))
    k_top = min(N_SEL, n_s)
    kb = jnp.moveaxis(k_s.reshape(B, n_s, SLC_BLOCK, G, dh), 3, 1)
    vb = jnp.moveaxis(v_s.reshape(B, n_s, SLC_BLOCK, G, dh), 3, 1)
    b_idx = jnp.arange(B)[:, None, None, None]
    g_idx = jnp.arange(G)[None, :, None, None]
    blk_ids = jnp.arange(n_s)
    n_ch = S // NSA_QCHUNK
    qc = jnp.moveaxis(q.reshape(B, n_ch, NSA_QCHUNK, G, R, dh), 1, 0)

    def chunk(args):
        i, qi = args
        qpos = i * NSA_QCHUNK + jnp.arange(NSA_QCHUNK)
        s_c = jnp.einsum('bqgrd,bcgd->bgrqc', qi, k_cmp).astype(jnp.float32) * scale
        p_c = masked_softmax(s_c, cmp_end[None, :] <= qpos[:, None])
        o_c = jnp.einsum('bgrqc,bcgd->bqgrd', p_c.astype(v_cmp.dtype), v_cmp)
        imp = jnp.einsum('bgrqc,cj->bgqj', p_c, overlap)
        forced = (blk_ids[None, :] == 0) | (blk_ids[None, :] == (qpos // SLC_BLOCK)[:, None])
        causal = blk_ids[None, :] * SLC_BLOCK <= qpos[:, None]
        score = jnp.where(causal, imp + jnp.where(forced, FORCE_BONUS, 0.0), NEG_INF)
        _, sel = lax.top_k(score, k_top)
        ks = kb[b_idx, g_idx, sel]
        vs = vb[b_idx, g_idx, sel]
        spos = sel[..., None] * SLC_BLOCK + jnp.arange(SLC_BLOCK)
        smask = (spos <= qpos[None, None, :, None, None])[:, :, None]
        s_s = jnp.einsum('bqgrd,bgqkld->bgrqkl', qi, ks).astype(jnp.float32) * scale
        s_s = jnp.where(smask, s_s, NEG_INF)
        p_s = jax.nn.softmax(s_s.reshape(s_s.shape[:4] + (-1,)), axis=-1).reshape(s_s.shape)
        o_s = jnp.einsum('bgrqkl,bgqkld->bqgrd', p_s.astype(vs.dtype), vs)
        return o_c, o_s

    o_c, o_s = lax.map(chunk, (jnp.arange(n_ch), qc))
    o_c = jnp.moveaxis(o_c, 0, 1).reshape(B, S, H, dh)
    o_s = jnp.moveaxis(o_s, 0, 1).reshape(B, S, H, dh)
    o_w = banded_attention(q, k_w, v_w, NSA_WINDOW)
    g = jax.nn.sigmoid(gate_logits.astype(jnp.float32)).reshape(B, S, H, 3).astype(q.dtype)
    o = g[..., 0:1] * o_c + g[..., 1:2] * o_s + g[..., 2:3] * o_w
    return o.reshape(B, S, H * dh)


def cross_attention(h, mem_n, w_q, w_kv, w_o):
    B, S, D = h.shape
    M = mem_n.shape[1]
    q = (h @ w_q).reshape(B, S, X_HEADS, X_HEAD_DIM)
    kv = (mem_n @ w_kv).reshape(B, M, 2, X_HEADS, X_HEAD_DIM)
    s = jnp.einsum('bqhd,bkhd->bhqk', q, kv[:, :, 0]).astype(jnp.float32) * (X_HEAD_DIM ** -0.5)
    p = jax.nn.softmax(s, axis=-1)
    o = jnp.einsum('bhqk,bkhd->bqhd', p.astype(h.dtype), kv[:, :, 1]).reshape(B, S, D)
    return o @ w_o


def setup_inputs(seed: int = 0) -> dict:
    key = jax.random.key(seed)
    keys = iter(jax.random.split(key, 48))
    f32 = jnp.float32
    L = DEPTH
    D = D_MODEL

    def dense(shape, fan_in):
        return jax.random.normal(next(keys), shape, f32) * (fan_in ** -0.5)

    def gain(shape):
        return 1.0 + 0.05 * jax.random.normal(next(keys), shape, f32)

    def small(shape, s=0.02):
        return s * jax.random.normal(next(keys), shape, f32)

    x = jax.random.normal(next(keys), (BATCH, SEQ, D), f32)
    mem = jax.random.normal(next(keys), (BATCH, MEM_LEN, D), f32)
    offset = jax.random.randint(next(keys), (BATCH, 1), 0, 4096, dtype=jnp.int32)
    positions = offset + jnp.arange(SEQ, dtype=jnp.int32)[None, :]
    cg = POOL_W // POOL_GROUPS
    return {
        'x': x, 'mem': mem, 'positions': positions,
        'ffn1_pre_g': gain((L, D)), 'ffn1_w_in': dense((L, D, 2 * D_FF), D),
        'ffn1_w_out': dense((L, D_FF, D), D_FF), 'ffn1_post_g': gain((L, D)),
        'mix_pre_g': gain((L, D)), 'w_in': dense((L, D, D_IN), D),
        'conv_w': dense((L, CONV_K, CONV_W), CONV_K), 'conv_b': small((L, CONV_W)),
        'conv_ln_g': gain((L, CONV_W)), 'conv_ln_b': small((L, CONV_W)),
        'pool_w': dense((L, POOL_GROUPS, cg, cg), cg), 'pool_scale': gain((L, POOL_W)),
        'cmp_k_pos': small((L, CMP_BLOCK, HEAD_DIM), 0.1),
        'cmp_k_w1': dense((L, CMP_BLOCK * HEAD_DIM, CMP_HIDDEN), CMP_BLOCK * HEAD_DIM),
        'cmp_k_b1': small((L, CMP_HIDDEN)), 'cmp_k_w2': dense((L, CMP_HIDDEN, HEAD_DIM), CMP_HIDDEN),
        'cmp_k_b2': small((L, HEAD_DIM)),
        'cmp_v_pos': small((L, CMP_BLOCK, HEAD_DIM), 0.1),
        'cmp_v_w1': dense((L, CMP_BLOCK * HEAD_DIM, CMP_HIDDEN), CMP_BLOCK * HEAD_DIM),
        'cmp_v_b1': small((L, CMP_HIDDEN)), 'cmp_v_w2': dense((L, CMP_HIDDEN, HEAD_DIM), CMP_HIDDEN),
        'cmp_v_b2': small((L, HEAD_DIM)),
        'swa_sinks': jax.random.normal(next(keys), (L, SWA_HEADS), f32),
        'w_branch': dense((L, N_BRANCH, BRANCH_W, D), BRANCH_W), 'w_out': dense((L, D, D), D),
        'mix_post_g': gain((L, D)),
        'x_pre_g': gain((L, D)), 'mem_g': gain((L, D)), 'w_xq': dense((L, D, D), D),
        'w_xkv': dense((L, D, 2 * D), D), 'w_xo': dense((L, D, D), D), 'x_post_g': gain((L, D)),
        'ffn2_pre_g': gain((L, D)), 'ffn2_w_in': dense((L, D, 2 * D_FF), D),
        'ffn2_w_out': dense((L, D_FF, D), D_FF), 'ffn2_post_g': gain((L, D)),
    }


def reference(x, mem, positions,
              ffn1_pre_g, ffn1_w_in, ffn1_w_out, ffn1_post_g,
              mix_pre_g, w_in, conv_w, conv_b, conv_ln_g, conv_ln_b, pool_w, pool_scale,
              cmp_k_pos, cmp_k_w1, cmp_k_b1, cmp_k_w2, cmp_k_b2,
              cmp_v_pos, cmp_v_w1, cmp_v_b1, cmp_v_w2, cmp_v_b2,
              swa_sinks, w_branch, w_out, mix_post_g,
              x_pre_g, mem_g, w_xq, w_xkv, w_xo, x_post_g,
              ffn2_pre_g, ffn2_w_in, ffn2_w_out, ffn2_post_g):
    B, S, D = x.shape
    cos, sin = rope_tables(positions)
    splits = [int(o) for o in np.cumsum(IN_SIZES)[:-1]]
    for l in range(DEPTH):
        h = rms_norm(x, ffn1_pre_g[l])
        x = x + 0.5 * rms_norm(swiglu(h, ffn1_w_in[l], ffn1_w_out[l]), ffn1_post_g[l])
        h = rms_norm(x, mix_pre_g[l])
        u_conv, u_pool, q_nsa, kv_nsa, g_nsa, q_swa, kv_swa, u_gate = jnp.split(h @ w_in[l], splits, axis=-1)
        y_a = conv_module(u_conv, conv_w[l], conv_b[l], conv_ln_g[l], conv_ln_b[l])
        y_b = pool_mixer(u_pool, pool_w[l], pool_scale[l])
        q_n = apply_rope(q_nsa.reshape(B, S, NSA_HEADS, HEAD_DIM), cos, sin)
        kv_n = kv_nsa.reshape(B, S, 3, 2, NSA_KV, HEAD_DIM)
        k_n = apply_rope(kv_n[:, :, :, 0].reshape(B, S, 3 * NSA_KV, HEAD_DIM), cos, sin)
        k_n = k_n.reshape(B, S, 3, NSA_KV, HEAD_DIM)
        v_n = kv_n[:, :, :, 1]
        y_c = nsa_attention(q_n, k_n[:, :, 0], v_n[:, :, 0], k_n[:, :, 1], v_n[:, :, 1],
                            k_n[:, :, 2], v_n[:, :, 2], g_nsa,
                            cmp_k_pos[l], cmp_k_w1[l], cmp_k_b1[l], cmp_k_w2[l], cmp_k_b2[l],
                            cmp_v_pos[l], cmp_v_w1[l], cmp_v_b1[l], cmp_v_w2[l], cmp_v_b2[l])
        q_s = apply_rope(q_swa.reshape(B, S, SWA_HEADS, HEAD_DIM), cos, sin)
        kv_s = kv_swa.reshape(B, S, 2, SWA_KV, HEAD_DIM)
        k_s = apply_rope(kv_s[:, :, 0], cos, sin)
        y_d = banded_attention(q_s, k_s, kv_s[:, :, 1], SWA_WINDOW, swa_sinks[l]).reshape(B, S, BRANCH_W)
        gates = jax.nn.sigmoid(u_gate.reshape(B, S, N_BRANCH, D))
        branches = (y_a, y_b, y_c, y_d)
        merged = gates[:, :, 0] * (y_a @ w_branch[l, 0])
        for n in range(1, N_BRANCH):
            merged = merged + gates[:, :, n] * (branches[n] @ w_branch[l, n])
        x = x + rms_norm(merged @ w_out[l], mix_post_g[l])
        h = rms_norm(x, x_pre_g[l])
        mem_n = rms_norm(mem, mem_g[l])
        x = x + rms_norm(cross_attention(h, mem_n, w_xq[l], w_xkv[l], w_xo[l]), x_post_g[l])
        h = rms_norm(x, ffn2_pre_g[l])
        x = x + 0.5 * rms_norm(swiglu(h, ffn2_w_in[l], ffn2_w_out[l]), ffn2_post_g[l])
    return x
```

```python
import math
import numpy as np
import ml_dtypes
from contextlib import ExitStack
import concourse.bass as bass
import concourse.mybir as mybir
from concourse.bass_utils import run_bass_kernel_spmd

F32 = mybir.dt.float32
BF = mybir.dt.bfloat16
I32 = mybir.dt.int32
AF = mybir.ActivationFunctionType
ALU = mybir.AluOpType
AX = mybir.AxisListType

D = 1024
DC = 8
DFF = 2816
FC = 22
EPS = 1e-6
NG = 9
SPW = 217

ENGS = ['pe', 'act', 'dve', 'pool', 'sp']
SAME_ENGINE_SYNC = {'act', 'dve', 'pool'}
EPOCH = 20000
NSLOT = 8


class Res:
    __slots__ = ('w', 'rs', 'name', 'excl')

    def __init__(self, name='', excl=False):
        self.w = None
        self.rs = {}
        self.name = name
        self.excl = excl


class Op:
    __slots__ = ('eng', 'fn', 'deps', 'idx', 'tick', 'dma', 'sig', 'slot', 'sval', 'slotwait')

    def __init__(self, eng, fn, dma):
        self.eng = eng
        self.fn = fn
        self.dma = dma
        self.deps = {}
        self.sig = False
        self.tick = None
        self.slot = None
        self.sval = None
        self.slotwait = None


class Sched:
    def __init__(self):
        self.ops = {e: [] for e in ENGS}

    def add(self, eng, fn, reads=(), writes=(), dma=False):
        op = Op(eng, fn, dma)
        op.idx = len(self.ops[eng])
        self.ops[eng].append(op)
        deps = op.deps

        def dep(o):
            if o is op:
                return
            if o.dma:
                deps[('dma', id(o))] = o
            else:
                k = o.eng
                if k not in deps or deps[k].idx < o.idx:
                    deps[k] = o
        for r in reads:
            if r.w is not None:
                dep(r.w)
            if r.excl:
                for o in r.rs.values():
                    if o.dma or o.eng != eng:
                        dep(o)
        for w in writes:
            if w.w is not None:
                dep(w.w)
            for o in w.rs.values():
                dep(o)
        key = ('dma', id(op)) if dma else eng
        for r in reads:
            r.rs[key] = op
        for w in writes:
            w.w = op
            w.rs = {}
        return op

    def emit(self, nc):
        for e in ENGS:
            for op in self.ops[e]:
                for k, d in list(op.deps.items()):
                    if (not d.dma) and d.eng == op.eng and (not op.dma) and op.eng not in SAME_ENGINE_SYNC:
                        del op.deps[k]
                        continue
                    d.sig = True
        nsem = {}
        for e in ENGS:
            t = 0
            nd = 0
            for op in self.ops[e]:
                if op.dma:
                    op.slot = nd % NSLOT
                    op.sval = 16 * (nd // NSLOT + 1)
                    op.slotwait = 16 * (nd // NSLOT)
                    nd += 1
                elif op.sig:
                    t += 1
                    op.tick = t
            nsem[e] = (t // EPOCH + 1, nd > 0)
        with ExitStack() as es:
            csem = {}
            dsem = {}
            for e in ENGS:
                csem[e] = [es.enter_context(nc.semaphore(f"c_{e}_{i}")) for i in range(nsem[e][0])]
                if nsem[e][1]:
                    dsem[e] = [es.enter_context(nc.semaphore(f"d_{e}_{i}")) for i in range(NSLOT)]
            block = es.enter_context(nc.Block())
            handles = {'pe': 'tensor', 'act': 'scalar', 'dve': 'vector', 'pool': 'gpsimd', 'sp': 'sync'}

            def emit_engine(e, h):
                known = {}
                for op in self.ops[e]:
                    for k, d in op.deps.items():
                        if d.dma:
                            sem = dsem[d.eng][d.slot]
                            val = d.sval
                            kk = ('d', d.eng, d.slot)
                        else:
                            ep = (d.tick - 1) // EPOCH
                            sem = csem[d.eng][ep]
                            val = d.tick - ep * EPOCH
                            kk = ('c', d.eng, ep)
                        if known.get(kk, 0) >= val:
                            continue
                        known[kk] = val
                        h.wait_ge(sem, val)
                    if op.dma:
                        if op.slotwait > 0:
                            kk = ('d', e, op.slot)
                            if known.get(kk, 0) < op.slotwait:
                                known[kk] = op.slotwait
                                h.wait_ge(dsem[e][op.slot], op.slotwait)
                        ins = op.fn(h)
                        ins.then_inc(dsem[e][op.slot], 16)
                    else:
                        ins = op.fn(h)
                        if op.sig:
                            ep = (op.tick - 1) // EPOCH
                            ins.then_inc(csem[e][ep], 1)
                if e in dsem:
                    last = {}
                    for op in self.ops[e]:
                        if op.dma:
                            last[op.slot] = op.sval
                    for s, v in last.items():
                        h.wait_ge(dsem[e][s], v)

            for e in ENGS:
                if not self.ops[e]:
                    continue
                getattr(block, handles[e])(lambda h, e=e: emit_engine(e, h))


class T:
    def __init__(self, h, nres=1, name=''):
        self.h = h
        self.res = [Res(f"{name}{i}") for i in range(nres)]

    def __getitem__(self, k):
        return self.h[k]


class Ring:
    def __init__(self, tiles):
        self.tiles = tiles
        self.i = 0

    def next(self):
        t = self.tiles[self.i % len(self.tiles)]
        self.i += 1
        return t


class Builder:
    def __init__(self, cfg):
        self.cfg = cfg
        self.S = Sched()
        self.nc = bass.Bass("TRN2", target_bir_lowering=False)
        self.es = ExitStack()

    def sb(self, name, shape, dt, nres=1):
        h = self.es.enter_context(self.nc.sbuf_tensor('s_' + name, list(shape), dt))
        return T(h, nres, name)

    def ps(self, name, shape, dt=F32):
        h = self.es.enter_context(self.nc.psum_tensor('p_' + name, list(shape), dt))
        t = T(h, 1, name)
        t.res[0].excl = True
        return t

    def din(self, name, shape, dt):
        return self.nc.dram_tensor(name, list(shape), dt, kind="ExternalInput").ap()

    def dout(self, name, shape, dt):
        return self.nc.dram_tensor(name, list(shape), dt, kind="ExternalOutput").ap()

    def ar_reset(self):
        self.ar_off = 0
        self.ar_top = self.AR_BYTES

    def al_top(self, nbytes, dt=BF):
        n = (nbytes + 2047) // 2048 * 2048
        off = self.ar_top - n
        assert off >= self.ar_off, (off, self.ar_off)
        self.ar_top = off
        ap = self.arena.h[:, off // 2:(off + nbytes) // 2]
        if dt != BF:
            ap = ap.bitcast(dt)
        t = T(ap, 0)
        t.res = self.arena.res[off // 2048:(off + n) // 2048]
        return t

    def al(self, nbytes, dt=BF):
        off = self.ar_off
        n = (nbytes + 2047) // 2048 * 2048
        assert off + n <= self.ar_top, (off, n, self.ar_top)
        self.ar_off = off + n
        ap = self.arena.h[:, off // 2:(off + nbytes) // 2]
        if dt != BF:
            ap = ap.bitcast(dt)
        t = T(ap, 0)
        t.res = self.arena.res[off // 2048:(off + n) // 2048]
        return t

    def mm(self, out, lhsT, rhs, start, stop, reads, writes):
        self.S.add('pe', lambda e: e.matmul(out, lhsT, rhs, start=start, stop=stop), reads, writes)

    def tr(self, out, in_, ident, reads, writes):
        self.S.add('pe', lambda e: e.transpose(out, in_, ident), reads, writes)

    def act(self, out, in_, func, reads, writes, bias=None, scale=None):
        kw = {}
        if bias is not None:
            kw['bias'] = bias
        if scale is not None:
            kw['scale'] = scale
        self.S.add('act', lambda e: e.activation(out, in_, func, **kw), reads, writes)

    def dma(self, eng, out, in_, reads, writes):
        self.S.add(eng, lambda e: e.dma_start(out=out, in_=in_), reads, writes, dma=True)

    def v(self, eng, name, args, reads, writes, **kw):
        self.S.add(eng, lambda e: getattr(e, name)(*args, **kw), reads, writes)

    def build(self):
        cfg = self.cfg
        nc = self.nc
        NSEQ = cfg['nseq']
        L = cfg['layers']
        TT = cfg.get('T', 2048)
        self.TT = TT
        NT = TT // 128
        NCH = TT // 512
        phases = cfg.get('phases', ['ffn1', 'mix', 'xatt', 'ffn2'])

        x_d = self.din('x', [NSEQ, TT, D], F32)
        out_d = self.dout('out', [NSEQ, TT, D], F32)
        gains_d = self.din('gains', [128, L * NG * DC], F32)
        cst_bf_d = self.din('cst_bf', [128, 512], BF)
        ident_f_d = self.din('ident_f', [128, 128], F32)
        w = {}
        w['ffn1_w_in'] = self.din('ffn1_w_in', [L, D, 2 * DFF], F32)
        w['ffn1_w_out'] = self.din('ffn1_w_out', [L, DFF, D], F32)
        w['ffn2_w_in'] = self.din('ffn2_w_in', [L, D, 2 * DFF], F32)
        w['ffn2_w_out'] = self.din('ffn2_w_out', [L, DFF, D], F32)
        for n_, shp in [('w_xq', [L, D, D]), ('w_xkv', [L, D, 2 * D]), ('w_xo', [L, D, D])]:
            w[n_] = self.din(n_, shp, F32)
        self.mem_d = self.din('mem', [NSEQ, 256, D], F32)
        w['w_in'] = self.din('w_in', [L, D, 7704], F32)
        w['pool_w'] = self.din('pool_w', [L, 4, 128, 128], F32)
        for nm in ('k', 'v'):
            w[f'cmp_{nm}_w1'] = self.din(f'cmp_{nm}_w1', [L, 2048, 256], F32)
            w[f'cmp_{nm}_w2'] = self.din(f'cmp_{nm}_w2', [L, 256, 64], F32)
        w['cmp_posT'] = self.din('cmp_posT', [L, 2, 64, 32], F32)
        w['w_branch'] = self.din('w_branch', [L, 4, 512, D], F32)
        w['w_out'] = self.din('w_out', [L, D, D], F32)
        self.pos_d = self.din('positions', [NSEQ, TT], I32)
        masks_d = self.din('masks', [128, 18 * 128], BF)
        tkb_d = self.din('tkb', [128, NT * 64], F32)
        gcst_d = self.din('gcst', [128, 66], F32)
        spar_d = self.din('spar', [128, L * SPW], F32)
        ovl_d = self.din('ovl', [128, 32], BF)
        self.Eg_d = self.din('Eg', [64, 2 * TT], BF)
        self.xs_d = self.nc.dram_tensor('xs', [128, DC, TT], F32, kind="Internal").ap()
        self.xs_res = Res('xs')
        if cfg.get('debug'):
            self.dbg_d = self.dout('dbg', [128, 16 * TT], BF)
            self.dbg2_d = self.dout('dbg2', [64, TT], BF)
            self.dbg3_d = self.dout('dbg3', [128, NT * 64], F32)
        self.w = w

        self.xT = self.sb('xT', [128, DC, TT], F32, nres=NCH)
        self.gains = self.sb('gains', [128, L * NG * DC], F32)
        self.cst_bf = self.sb('cst_bf', [128, 512], BF)
        self.masks = self.sb('masks', [128, 2 * 128], BF)
        self.gcst = self.sb('gcst', [128, 66], F32)
        self.spar = self.sb('spar', [128, L * SPW], F32)
        self.ovl = self.sb('ovl', [128, 32], BF)
        self.ident_f = self.sb('ident_f', [128, 128], F32)
        self.ident = self.cst_bf[:, 0:128]
        self.onesD = self.cst_bf[:, 128:256]
        self.onesC = self.cst_bf[:, 256:384]
        self.Rm = self.cst_bf[:, 384:512]
        self.AR_BYTES = 134 * 1024
        self.arena = self.sb('arena', [128, self.AR_BYTES // 2], BF, nres=self.AR_BYTES // 2048)
        self.ar_off = 0
        self.pb = [self.ps(f'pb{i}', [128, 512], F32) for i in range(7)]
        self.pbT = self.ps('pbT', [128, 1024], BF)
        self.cnt = {}

        S = self.S
        self.dma('sp', self.gains[:, :], gains_d[:, :], [], self.gains.res)
        self.dma('sp', self.cst_bf[:, :], cst_bf_d[:, :], [], self.cst_bf.res)
        self.dma('sp', self.ident_f[:, :], ident_f_d[:, :], [], self.ident_f.res)
        self.dma('sp', self.masks[:, :], masks_d[:, 0:256], [], self.masks.res)
        self.masks_d = masks_d
        self.tkb_d = tkb_d
        self.dma('sp', self.gcst[:, :], gcst_d[:, :], [], self.gcst.res)
        self.dma('sp', self.spar[:, :], spar_d[:, :], [], self.spar.res)
        self.dma('sp', self.ovl[:, :], ovl_d[:, :], [], self.ovl.res)

        for s in range(NSEQ):
            self.load_x(x_d, s)
            for l in range(L):
                for ph in phases:
                    if ph == 'ffn1':
                        self.ffn(l, 'ffn1', 0, 1)
                    elif ph == 'ffn2':
                        self.ffn(l, 'ffn2', 6, 7)
                    elif ph == 'xatt':
                        self.xatt(l, s)
                    elif ph == 'mix':
                        self.mixer(l, s)
            self.store_x(out_d, s)
        S.emit(nc)
        self.es.close()
        return nc

    def rr(self, key, n):
        i = self.cnt.get(key, 0)
        self.cnt[key] = i + 1
        return i % n

    def load_x(self, x_d, s):
        NT = self.TT // 128
        self.ar_reset()
        self.xin = [self.al(D * 4, F32) for i in range(2)]
        for t in range(NT):
            xi = self.xin[self.rr('xin', 2)]
            self.dma('sp', xi[:, :], x_d[s, t * 128:(t + 1) * 128, :], [], xi.res)
            for half in range(2):
                bank = self.pb[self.rr('pbx', 2)]
                for c4 in range(4):
                    c = half * 4 + c4
                    self.tr(bank[:, c4 * 128:(c4 + 1) * 128], xi[:, c * 128:(c + 1) * 128], self.ident_f[:, :],
                            xi.res + self.ident_f.res, bank.res)
                dst = self.xT[:, half * 4:half * 4 + 4, t * 128:(t + 1) * 128]
                src = bank[:, :].rearrange("p (c t) -> p c t", c=4)
                eng = 'act' if half == 0 else 'dve'
                if eng == 'act':
                    self.act(dst, src, AF.Copy, bank.res, [self.xT.res[t // 4]])
                else:
                    self.v('dve', 'tensor_copy', (dst, src), bank.res, [self.xT.res[t // 4]])

    def store_x(self, out_d, s):
        NT = self.TT // 128
        self.ar_reset()
        self.xin = [self.al(D * 4, F32) for i in range(2)]
        for t in range(NT):
            xi = self.xin[self.rr('xin', 2)]
            for half in range(2):
                bank = self.pb[self.rr('pbx', 2)]
                for c4 in range(4):
                    c = half * 4 + c4
                    self.tr(bank[:, c4 * 128:(c4 + 1) * 128], self.xT[:, c, t * 128:(t + 1) * 128], self.ident_f[:, :],
                            [self.xT.res[t // 4]] + self.ident_f.res, bank.res)
                dst = xi[:, half * 512:(half + 1) * 512]
                if half == 0:
                    self.act(dst, bank[:, :], AF.Copy, bank.res, xi.res)
                else:
                    self.v('dve', 'tensor_copy', (dst, bank[:, :]), bank.res, xi.res)
            self.dma('sp', out_d[s, t * 128:(t + 1) * 128, :], xi[:, :], xi.res, [])

    def rsqrt_eps(self, rstd, stat):
        self.v('dve', 'tensor_scalar', (rstd[:, :], stat[:, :], EPS, None, ALU.add), stat.res, rstd.res)
        self.act(rstd[:, :], rstd[:, :], AF.Sqrt, rstd.res, rstd.res)
        self.v('dve', 'reciprocal', (rstd[:, :], rstd[:, :]), rstd.res, rstd.res)

    def gain(self, l, gi):
        o = (l * NG + gi) * DC
        return self.gains[:, o:o + DC]

    def rms_pre(self, src, src_res, g, dst, dst_res, sq, sq_res):
        i = self.rr('rstd', 2)
        rstd = self.rstd[i]
        stat = self.pb[6]
        self.act(sq, src, AF.Square, src_res, sq_res)
        for c in range(DC):
            self.mm(stat[:, :], self.onesD, sq[:, c, :], c == 0, c == DC - 1, sq_res + self.cst_bf.res, stat.res)
        self.rsqrt_eps(rstd, stat)
        for c in range(DC):
            self.v('dve', 'scalar_tensor_tensor', (dst[:, c, :], src[:, c, :], g[:, c:c + 1], rstd[:, :], ALU.mult, ALU.mult),
                   src_res + rstd.res + self.gains.res, dst_res)

    def rms_post_add(self, y, y_res, sq, sq_res, g, alpha, tc):
        i = self.rr('rstd', 2)
        rstd = self.rstd[i]
        stat = self.pb[6]
        for c in range(DC):
            self.mm(stat[:, :], self.onesD, sq[:, c, :], c == 0, c == DC - 1, sq_res + self.cst_bf.res, stat.res)
        self.rsqrt_eps(rstd, stat)
        xr = [self.xT.res[tc]]
        for c in range(DC):
            tmp = self.tmpf[self.rr('tmpf', 2)]
            self.v('dve', 'scalar_tensor_tensor', (tmp[:, :], y[:, c, :], g[:, c:c + 1], rstd[:, :], ALU.mult, ALU.mult),
                   y_res + rstd.res + self.gains.res, tmp.res)
            xs = self.xT[:, c, tc * 512:(tc + 1) * 512]
            self.v('dve', 'scalar_tensor_tensor', (xs, tmp[:, :], float(alpha), xs, ALU.mult, ALU.add),
                   tmp.res + xr, xr)

    def ffn(self, l, name, gpre, gpost):
        w_in = self.w[name + '_w_in'][l].rearrange("(kc p) n -> p kc n", p=128)
        w_out = self.w[name + '_w_out'][l].rearrange("(kc p) n -> p kc n", p=128)
        NH = self.TT // 1024
        self.ar_reset()
        hT = [self.al(DC * 512 * 2) for j in range(2)]
        gT = [self.al(FC * 512 * 2) for j in range(2)]
        y = [self.al(DC * 512 * 4, F32) for j in range(2)]
        wra = [self.al(DC * 256 * 2) for j in range(3)]
        wrb = [self.al(FC * 128 * 2) for j in range(2)]
        self.alloc_small()
        hv = [t[:, :].rearrange("p (c t) -> p c t", c=DC) for t in hT]
        gv = [t[:, :].rearrange("p (f t) -> p f t", f=FC) for t in gT]
        yv = [t[:, :].rearrange("p (c t) -> p c t", c=DC) for t in y]
        for half in range(NH):
            for j in range(2):
                tc = half * 2 + j
                self.rms_pre(self.xT[:, :, tc * 512:(tc + 1) * 512], [self.xT.res[tc]], self.gain(l, gpre),
                             hv[j], hT[j].res, hv[j], hT[j].res)
            for f in range(FC):
                wt = wra[self.rr('wra', 3)]
                wv = wt[:, :].rearrange("p (c n) -> p c n", c=DC)
                self.dma('pool', wv[:, :, 0:128], w_in[:, :, f * 128:(f + 1) * 128], [], wt.res)
                self.dma('pool', wv[:, :, 128:256], w_in[:, :, DFF + f * 128:DFF + (f + 1) * 128], [], wt.res)
                for j in range(2):
                    pa = self.pb[self.rr('pa', 2)]
                    pbk = self.pb[2 + self.rr('pb', 2)]
                    for c in range(DC):
                        self.mm(pa[:, :], wv[:, c, 0:128], hv[j][:, c, :], c == 0, c == DC - 1,
                                wt.res + hT[j].res, pa.res)
                    for c in range(DC):
                        self.mm(pbk[:, :], wv[:, c, 128:256], hv[j][:, c, :], c == 0, c == DC - 1,
                                wt.res + hT[j].res, pbk.res)
                    sa = self.tmpb[self.rr('tmpb', 4)]
                    self.act(sa[:, :], pa[:, :], AF.Silu, pa.res, sa.res)
                    self.v('dve', 'tensor_tensor', (gv[j][:, f, :], sa[:, :], pbk[:, :], ALU.mult),
                           sa.res + pbk.res, gT[j].res)
            for d in range(DC):
                wt = wrb[self.rr('wrb', 2)]
                wv = wt[:, :].rearrange("p (f n) -> p f n", f=FC)
                self.dma('pool', wv, w_out[:, :, d * 128:(d + 1) * 128], [], wt.res)
                for j in range(2):
                    py = self.pb[4 + self.rr('py', 2)]
                    for f in range(FC):
                        self.mm(py[:, :], wv[:, f, :], gv[j][:, f, :], f == 0, f == FC - 1,
                                wt.res + gT[j].res, py.res)
                    self.act(yv[j][:, d, :], py[:, :], AF.Copy, py.res, y[j].res)
                    self.act(hv[j][:, d, :], py[:, :], AF.Square, py.res, hT[j].res)
            for j in range(2):
                tc = half * 2 + j
                self.rms_post_add(yv[j], y[j].res, hv[j], hT[j].res, self.gain(l, gpost), 0.5, tc)

    def wload(self, dst_t, dst_ap, src_ap):
        self.dma('pool', dst_ap, src_ap, [], dst_t.res)

    def xatt(self, l, s):
        TT = self.TT
        NCH = TT // 512
        wq_d = self.w['w_xq'][l].rearrange("(kc p) n -> p kc n", p=128)
        wkv_d = self.w['w_xkv'][l].rearrange("(kc p) n -> p kc n", p=128)
        wo_d = self.w['w_xo'][l].rearrange("(kc p) n -> p kc n", p=128)
        self.ar_reset()
        wq = self.al(DC * D * 2); wqv = wq[:, :].rearrange("p (c n) -> p c n", c=DC)
        wo = self.al(DC * D * 2); wov = wo[:, :].rearrange("p (c n) -> p c n", c=DC)
        self.wload(wq, wqv, wq_d)
        self.wload(wo, wov, wo_d)
        mnT = self.al(DC * 256 * 2); mnTv = mnT[:, :].rearrange("p (c t) -> p c t", c=DC)
        kxT = self.al(DC * 256 * 2); kxTv = kxT[:, :].rearrange("p (c t) -> p c t", c=DC)
        VW = 4 * 257
        vx = self.al(2 * VW * 2); vxv = vx[:, :].rearrange("p (m h e) -> p m h e", m=2, h=4)
        hT = self.al(DC * 512 * 2); hv = hT[:, :].rearrange("p (c t) -> p c t", c=DC)
        qT = self.al(DC * 512 * 2); qv = qT[:, :].rearrange("p (c t) -> p c t", c=DC)
        otok = self.al(4 * D * 2); otv = otok[:, :].rearrange("p (i n) -> p i n", i=4)
        oT = self.al(DC * 512 * 2); oTv = oT[:, :].rearrange("p (c t) -> p c t", c=DC)
        y = self.al(DC * 512 * 4, F32); yv = y[:, :].rearrange("p (c t) -> p c t", c=DC)
        wk = [self.al(DC * 128 * 2) for i in range(2)]
        wv_ = [self.al(DC * 512 * 2) for i in range(1)]
        xi = self.al(2 * D * 4, F32); xiv = xi[:, :].rearrange("p (m n) -> p m n", m=2)
        memT = y; memTv = y[:, 0:DC * 256].rearrange("p (c t) -> p c t", c=DC)
        self.alloc_small()
        self.dma('sp', xiv, self.mem_d[s].rearrange("(m p) n -> p m n", p=128), [], xi.res)
        for m in range(2):
            for half in range(2):
                bank = self.pb[self.rr('pbx', 2)]
                for c4 in range(4):
                    c = half * 4 + c4
                    self.tr(bank[:, c4 * 128:(c4 + 1) * 128], xiv[:, m, c * 128:(c + 1) * 128], self.ident_f[:, :],
                            xi.res + self.ident_f.res, bank.res)
                self.act(memTv[:, half * 4:half * 4 + 4, m * 128:(m + 1) * 128],
                         bank[:, :].rearrange("p (c t) -> p c t", c=4), AF.Copy, bank.res, memT.res)
        stat = self.pb[6]
        rstd = self.rstd[self.rr('rstd', 2)]
        self.act(mnTv, memTv, AF.Square, memT.res, mnT.res)
        for c in range(DC):
            self.mm(stat[:, 0:256], self.onesD, mnTv[:, c, :], c == 0, c == DC - 1, mnT.res + self.cst_bf.res, stat.res)
        self.v('dve', 'tensor_scalar', (rstd[:, 0:256], stat[:, 0:256], EPS, None, ALU.add), stat.res, rstd.res)
        self.act(rstd[:, 0:256], rstd[:, 0:256], AF.Sqrt, rstd.res, rstd.res)
        self.v('dve', 'reciprocal', (rstd[:, 0:256], rstd[:, 0:256]), rstd.res, rstd.res)
        g = self.gain(l, 8)
        for c in range(DC):
            self.v('dve', 'scalar_tensor_tensor', (mnTv[:, c, :], memTv[:, c, :], g[:, c:c + 1], rstd[:, 0:256], ALU.mult, ALU.mult),
                   memT.res + rstd.res + self.gains.res, mnT.res)
        for j in range(DC):
            wt = wk[self.rr('wk', 2)]
            wtv = wt[:, :].rearrange("p (c n) -> p c n", c=DC)
            self.wload(wt, wtv, wkv_d[:, :, j * 128:(j + 1) * 128])
            pa = self.pb[self.rr('pa', 2)]
            for c in range(DC):
                self.mm(pa[:, 0:256], wtv[:, c, :], mnTv[:, c, :], c == 0, c == DC - 1, wt.res + mnT.res, pa.res)
            self.act(kxTv[:, j, :], pa[:, 0:256], AF.Copy, pa.res, kxT.res)
        self.v('dve', 'memset', (vxv[:, :, :, 256:257], 1.0), [], vx.res)
        for n2 in range(2):
            wt = wv_[0]
            wtv = wt[:, :].rearrange("p (c n) -> p c n", c=DC)
            self.wload(wt, wtv, wkv_d[:, :, D + n2 * 512:D + (n2 + 1) * 512])
            for m in range(2):
                pa = self.pb[self.rr('pa', 2)]
                for c in range(DC):
                    self.mm(pa[:, :], mnTv[:, c, m * 128:(m + 1) * 128], wtv[:, c, :], c == 0, c == DC - 1,
                            wt.res + mnT.res, pa.res)
                self.act(vxv[:, m, 2 * n2:2 * n2 + 2, 0:256], pa[:, :].rearrange("p (h e) -> p h e", h=2), AF.Copy,
                         pa.res, vx.res)
        for tc in range(NCH):
            self.rms_pre(self.xT[:, :, tc * 512:(tc + 1) * 512], [self.xT.res[tc]], self.gain(l, 4),
                         hv, hT.res, hv, hT.res)
            for j in range(DC):
                py = self.pb[4 + self.rr('py', 2)]
                for c in range(DC):
                    self.mm(py[:, :], wqv[:, c, j * 128:(j + 1) * 128], hv[:, c, :], c == 0, c == DC - 1,
                            wq.res + hT.res, py.res)
                self.act(qv[:, j, :], py[:, :], AF.Copy, py.res, qT.res)
            for h in range(4):
                pts = []
                for m in range(2):
                    pa = self.pb[self.rr('pa', 2)]
                    for jj in range(2):
                        self.mm(pa[:, :], kxTv[:, 2 * h + jj, m * 128:(m + 1) * 128], qv[:, 2 * h + jj, :], jj == 0, jj == 1,
                                kxT.res + qT.res, pa.res)
                    pt = self.tmpb[self.rr('tmpb', 4)]
                    self.act(pt[:, :], pa[:, :], AF.Exp, pa.res, pt.res, scale=1.0 / 16.0)
                    pts.append(pt)
                for i in range(4):
                    po = self.pb[2 + self.rr('pb', 2)]
                    for m in range(2):
                        self.mm(po[:, 0:257], pts[m][:, i * 128:(i + 1) * 128], vxv[:, m, h, :], m == 0, m == 1,
                                pts[m].res + vx.res, po.res)
                    k = self.rr('sm', 8)
                    rd = self.sm[:, k:k + 1]
                    self.v('dve', 'reciprocal', (rd, po[:, 256:257]), po.res, self.sm.res)
                    self.v('dve', 'tensor_scalar', (otv[:, i, h * 256:(h + 1) * 256], po[:, 0:256], rd, None, ALU.mult),
                           po.res + self.sm.res, otok.res)
            for c in range(DC):
                pT = self.pbT
                for i in range(4):
                    self.tr(pT[:, i * 128:(i + 1) * 128], otv[:, i, c * 128:(c + 1) * 128], self.ident,
                            otok.res + self.cst_bf.res, pT.res)
                if c % 2 == 0:
                    self.act(oTv[:, c, :], pT[:, 0:512], AF.Copy, pT.res, oT.res)
                else:
                    self.v('dve', 'tensor_copy', (oTv[:, c, :], pT[:, 0:512]), pT.res, oT.res)
            for d in range(DC):
                py = self.pb[4 + self.rr('py', 2)]
                for c in range(DC):
                    self.mm(py[:, :], wov[:, c, d * 128:(d + 1) * 128], oTv[:, c, :], c == 0, c == DC - 1,
                            wo.res + oT.res, py.res)
                self.act(yv[:, d, :], py[:, :], AF.Copy, py.res, y.res)
                self.act(hv[:, d, :], py[:, :], AF.Square, py.res, hT.res)
            self.rms_post_add(yv, y.res, hv, hT.res, self.gain(l, 5), 1.0, tc)

    def mark(self):
        return self.ar_off

    def release(self, m, top=False):
        self.ar_off = m
        if top:
            self.ar_top = self.AR_BYTES

    def spv(self, l, off, n=1):
        o = l * SPW + off
        return self.spar[:, o:o + n]

    def wtile(self, W, cols, ring='wr', nring=3, width=128):
        tiles = self.rings[ring]
        wt = tiles[self.rr(ring, len(tiles))]
        wtv = wt[:, :].rearrange("p (c n) -> p c n", c=DC)
        for (o, c0, n) in cols:
            self.wload(wt, wtv[:, :, o:o + n], W[:, :, c0:c0 + n])
        return wt, wtv

    def proj_fm(self, wt, wtv, hv, hres, tc, bank, m0=0, m1=128, ncols=512):
        for c in range(DC):
            self.mm(bank[0:m1 - m0, 0:ncols], wtv[:, c, m0:m1], hv[:, c, tc * 512:tc * 512 + ncols], c == 0, c == DC - 1,
                    wt.res + hres, bank.res)

    def mixer(self, l, s):
        TT = self.TT
        NCH = TT // 512
        NT = TT // 128
        W = self.w['w_in'][l].rearrange("(kc p) n -> p kc n", p=128)
        self.ar_reset()
        hT = self.al(DC * TT * 2)
        hv = hT[:, :].rearrange("p (c t) -> p c t", c=DC)
        self.alloc_small()
        self.rings = {'wr': [self.al(DC * 128 * 2) for i in range(3)]}
        base = self.mark()
        for tc in range(NCH):
            sl = slice(tc * 512, (tc + 1) * 512)
            self.rms_pre(self.xT[:, :, sl], [self.xT.res[tc]], self.gain(l, 2), hv[:, :, sl], hT.res, hv[:, :, sl], hT.res)
        self.dma('sp', self.xs_d[:, :, :], self.xT[:, :, :], self.xT.res, [self.xs_res])
        ybf = self.xT.h[:, :, :].rearrange("p c t -> p (c t)").bitcast(BF)
        yv = [ybf[:, n * 4 * TT:(n + 1) * 4 * TT].rearrange("p (c t) -> p c t", c=4) for n in range(4)]
        yres = self.xT.res
        br = self.cfg.get('branches', 'abcd')
        if 'a' in br:
            self.mix_conv(l, W, hv, hT.res, yv[0], yres)
            self.release(base, True)
        if 'b' in br:
            self.mix_pool(l, W, hv, hT.res, yv[1], yres)
            self.release(base, True)
        if 'd' in br:
            self.mix_swa(l, s, W, hv, hT.res, yv[3], yres)
            self.release(base, True)
        if 'c' in br:
            self.mix_nsa(l, s, W, hv, hT.res, yv[2], yres)
            self.release(base, True)
        if self.cfg.get('debug'):
            self.dma('sp', self.dbg_d[:, :], ybf, yres, [])
        if self.cfg.get('merge', True):
            self.mix_merge(l, hT, hv, yv, yres)
        else:
            self.dma('sp', self.xT[:, :, :], self.xs_d[:, :, :], [self.xs_res], self.xT.res)

    def mix_conv(self, l, W, hv, hres, yout, yres):
        TT = self.TT
        NCH = TT // 512
        PAD = 32
        vt = self.al((PAD + TT) * 4, F32)
        yc = [self.al(TT * 4, F32) for c in range(4)]
        ybf = self.al(4 * 512 * 2)
        ysq = self.al(4 * 512 * 2)
        ybv = ybf[:, :].rearrange("p (c t) -> p c t", c=4)
        ysv = ysq[:, :].rearrange("p (c t) -> p c t", c=4)
        self.tmpf2 = [self.al(2048, F32) for _ in range(2)]
        self.v('dve', 'memset', (vt[:, 0:PAD], 0.0), [], vt.res)
        for c in range(4):
            wa, wav = self.wtile(W, [(0, c * 128, 128)])
            wg, wgv = self.wtile(W, [(0, 512 + c * 128, 128)])
            for tc in range(NCH):
                pa = self.pb[self.rr('pa', 2)]
                pg = self.pb[2 + self.rr('pb', 2)]
                self.proj_fm(wa, wav, hv, hres, tc, pa)
                self.proj_fm(wg, wgv, hv, hres, tc, pg)
                sg = self.tmpf[self.rr('tmpf', 2)]
                self.act(sg[:, :], pg[:, :], AF.Sigmoid, pg.res, sg.res)
                self.v('dve', 'tensor_tensor', (vt[:, PAD + tc * 512:PAD + (tc + 1) * 512], pa[:, :], sg[:, :], ALU.mult),
                       pa.res + sg.res, vt.res)
            acc = yc[c]
            cw = self.spv(l, c * 31, 31)
            cb = self.spv(l, 124 + c)
            self.v('dve', 'tensor_scalar', (acc[:, :], vt[:, PAD - 30:PAD - 30 + TT], cw[:, 0:1], cb, ALU.mult, ALU.add),
                   vt.res + self.spar.res, acc.res)
            for k in range(1, 31):
                o = PAD - 30 + k
                self.v('dve', 'scalar_tensor_tensor', (acc[:, :], vt[:, o:o + TT], cw[:, k:k + 1], acc[:, :], ALU.mult, ALU.add),
                       vt.res + self.spar.res + acc.res, acc.res)
        for tc in range(NCH):
            sl = slice(tc * 512, (tc + 1) * 512)
            for c in range(4):
                self.act(ybv[:, c, :], yc[c][:, sl], AF.Copy, yc[c].res, ybf.res)
                self.act(ysv[:, c, :], yc[c][:, sl], AF.Square, yc[c].res, ysq.res)
            pm = self.pb[self.rr('pa', 2)]
            p2 = self.pb[2 + self.rr('pb', 2)]
            for c in range(4):
                self.mm(pm[:, :], self.onesC, ybv[:, c, :], c == 0, c == 3, ybf.res + self.cst_bf.res, pm.res)
            for c in range(4):
                self.mm(p2[:, :], self.onesC, ysv[:, c, :], c == 0, c == 3, ysq.res + self.cst_bf.res, p2.res)
            mean = self.tmpf[self.rr('tmpf', 2)]
            rstd = self.rstd[self.rr('rstd', 2)]
            self.act(mean[:, :], pm[:, :], AF.Copy, pm.res, mean.res)
            self.v('dve', 'tensor_tensor', (rstd[:, :], mean[:, :], mean[:, :], ALU.mult), mean.res, rstd.res)
            self.v('dve', 'tensor_tensor', (rstd[:, :], p2[:, :], rstd[:, :], ALU.subtract), p2.res + rstd.res, rstd.res)
            self.v('dve', 'tensor_scalar', (rstd[:, :], rstd[:, :], EPS, None, ALU.add), rstd.res, rstd.res)
            self.act(rstd[:, :], rstd[:, :], AF.Sqrt, rstd.res, rstd.res)
            self.v('dve', 'reciprocal', (rstd[:, :], rstd[:, :]), rstd.res, rstd.res)
            for c in range(4):
                t = self.tmpf2[self.rr('tmpf2', 2)]
                self.v('dve', 'tensor_tensor', (t[:, :], yc[c][:, sl], mean[:, :], ALU.subtract), yc[c].res + mean.res, t.res)
                self.v('dve', 'tensor_tensor', (t[:, :], t[:, :], rstd[:, :], ALU.mult), t.res + rstd.res, t.res)
                self.act(yout[:, c, sl], t[:, :], AF.Silu, t.res + self.spar.res, yres,
                         scale=self.spv(l, 128 + c), bias=self.spv(l, 132 + c))

    def mix_pool(self, l, W, hv, hres, yout, yres):
        TT = self.TT
        NCH = TT // 512
        PAD = 16
        up = self.al((PAD + TT) * 4, F32)
        A = self.al((PAD + TT) * 4, F32)
        B = self.al((PAD + TT) * 4, F32)
        rc = self.al(TT * 4, F32)
        dB = self.al(TT * 2)
        wp = self.al(4 * 128 * 2)
        wpv = wp[:, :].rearrange("p (g n) -> p g n", g=4)
        self.wload(wp, wpv, self.w['pool_w'][l].rearrange("g p n -> p g n"))
        for t_ in (up, A, B):
            self.v('dve', 'memset', (t_[:, 0:PAD], 0.0), [], t_.res)
        for c in range(4):
            win = 2 << c
            wt, wtv = self.wtile(W, [(0, 1024 + c * 128, 128)])
            for tc in range(NCH):
                pa = self.pb[self.rr('pa', 2)]
                self.proj_fm(wt, wtv, hv, hres, tc, pa)
                self.act(up[:, PAD + tc * 512:PAD + (tc + 1) * 512], pa[:, :], AF.Copy, pa.res, up.res)
            src = up
            sh = 1
            bufs = [A, B]
            bi = 0
            while sh < win:
                dst = bufs[bi]
                bi ^= 1
                self.v('dve', 'tensor_tensor', (dst[:, PAD:PAD + TT], src[:, PAD:PAD + TT], src[:, PAD - sh:PAD - sh + TT], ALU.add),
                       src.res, dst.res)
                src = dst
                sh *= 2
            self.v('dve', 'memset', (rc[:, :], 1.0 / win), [], rc.res)
            self.v('dve', 'tensor_copy', (rc[:, 0:16], self.gcst[:, 2 + c * 16:2 + (c + 1) * 16]), self.gcst.res + rc.res, rc.res)
            self.v('dve', 'tensor_tensor', (src[:, PAD:PAD + TT], src[:, PAD:PAD + TT], rc[:, :], ALU.mult), src.res + rc.res, src.res)
            self.v('dve', 'tensor_tensor', (dB[:, :], src[:, PAD:PAD + TT], up[:, PAD:PAD + TT], ALU.subtract), src.res + up.res, dB.res)
            for tc in range(NCH):
                pa = self.pb[self.rr('pa', 2)]
                self.mm(pa[:, :], wpv[:, c, :], dB[:, tc * 512:(tc + 1) * 512], True, True, wp.res + dB.res, pa.res)
                self.act(yout[:, c, tc * 512:(tc + 1) * 512], pa[:, :], AF.Copy, pa.res + self.spar.res, yres,
                         scale=self.spv(l, 136 + c))

    def rope_tables(self, s):
        TT = self.TT
        C = self.al_top(TT * 4, F32)
        Sg = self.al_top(TT * 4, F32)
        m = self.mark()
        posi = self.al(TT * 4, I32)
        u = self.al(TT * 4, F32)
        nf = self.al(TT * 4, F32)
        self.dma('sp', posi[:, :], self.pos_d[s].partition_broadcast(128), [], posi.res)
        posf = nf
        for which, dst in ((0, Sg), (1, C)):
            self.v('dve', 'tensor_copy', (posf[:, :], posi[:, :]), posi.res, posf.res)
            self.v('dve', 'tensor_scalar', (u[:, :], posf[:, :], self.gcst[:, 0:1], 0.25 * which, ALU.mult, ALU.add),
                   posf.res + self.gcst.res, u.res)
            nib = nf[:, :].bitcast(I32)
            self.v('dve', 'tensor_copy', (nib, u[:, :]), u.res, nf.res)
            self.v('dve', 'tensor_copy', (nf[:, :], nib), nf.res, nf.res)
            self.v('dve', 'tensor_tensor', (u[:, :], u[:, :], nf[:, :], ALU.subtract), u.res + nf.res, u.res)
            self.v('dve', 'tensor_scalar', (nf[:, :], u[:, :], 0.5, None, ALU.is_gt), u.res, nf.res)
            self.v('dve', 'tensor_tensor', (u[:, :], u[:, :], nf[:, :], ALU.subtract), u.res + nf.res, u.res)
            self.v('dve', 'tensor_scalar', (u[:, :], u[:, :], 0.4999999, -0.4999999, ALU.min, ALU.max), u.res, u.res)
            if which == 0:
                self.act(dst[:, :], u[:, :], AF.Sin, u.res + self.gcst.res, dst.res, scale=self.gcst[:, 1:2])
            else:
                self.act(dst[:, :], u[:, :], AF.Sin, u.res, dst.res, scale=2.0 * math.pi)
        self.release(m)
        return C, Sg

    def rope(self, pa, dst, dst_res, C, Sg, sl, P=128):
        qb = self.tmpb[self.rr('tmpb', 4)]
        self.act(qb[0:P, :], pa[0:P, :], AF.Copy, pa.res, qb.res)
        pr = self.pb[4 + self.rr('py', 2)]
        self.mm(pr[0:P, :], self.Rm[0:P, 0:P], qb[0:P, :], True, True, qb.res + self.cst_bf.res, pr.res)
        t1 = self.tmpf[self.rr('tmpf', 2)]
        t2 = self.tmpf[self.rr('tmpf', 2)]
        self.v('dve', 'tensor_tensor', (t1[0:P, :], pa[0:P, :], C[0:P, sl], ALU.mult), pa.res + C.res, t1.res)
        self.v('dve', 'tensor_tensor', (t2[0:P, :], pr[0:P, :], Sg[0:P, sl], ALU.mult), pr.res + Sg.res, t2.res)
        self.v('dve', 'tensor_tensor', (dst, t1[0:P, :], t2[0:P, :], ALU.add), t1.res + t2.res, dst_res)

    def tok_proj(self, wt, wtv, hv, hres, t, bank, ncols):
        for c in range(DC):
            self.mm(bank[:, 0:ncols], hv[:, c, t * 128:(t + 1) * 128], wtv[:, c, 0:ncols], c == 0, c == DC - 1,
                    wt.res + hres, bank.res)

    def attn_tile(self, qv, qres, kT, kres, i, g, blocks, scale, Wd, fin):
        r0 = g * 64
        po = [self.pb[2 + hh] for hh in range(4)]
        nb = len(blocks)
        for bi, (j, Vap, vres, masks) in enumerate(blocks):
            pa = self.pb[self.rr('pa', 2)]
            self.mm(pa[:, :].rearrange("p (h q) -> p h q", h=4), kT[r0:r0 + 64, j * 128:(j + 1) * 128],
                    qv[r0:r0 + 64, :, i * 128:(i + 1) * 128], True, True, kres + qres, pa.res)
            pt = self.tmpb[self.rr('tmpb', 4)]
            self.act(pt[:, :], pa[:, :], AF.Exp, pa.res, pt.res, scale=scale)
            ptv = pt[:, :].rearrange("p (h q) -> p h q", h=4)
            for mk in masks:
                m_ap, mres = mk() if callable(mk) else mk
                self.v('dve', 'tensor_tensor', (ptv, ptv, m_ap.unsqueeze(1).to_broadcast([128, 4, 128]), ALU.mult),
                       pt.res + mres, pt.res)
            for hh in range(4):
                self.mm(po[hh][:, 0:Wd], pt[:, hh * 128:(hh + 1) * 128], Vap, bi == 0, bi == nb - 1,
                        pt.res + vres, po[hh].res)
        fin(po)

    def o_to_y(self, o_b, yout, yres, i):
        pT = self.pbT
        for c in range(4):
            self.tr(pT[:, c * 128:(c + 1) * 128], o_b[:, c * 128:(c + 1) * 128], self.ident, o_b.res + self.cst_bf.res, pT.res)
        self.act(yout[:, :, i * 128:(i + 1) * 128], pT[:, 0:512].rearrange("p (c t) -> p c t", c=4), AF.Copy, pT.res, yres)

    def mix_swa(self, l, s, W, hv, hres, yout, yres):
        TT = self.TT
        NCH = TT // 512
        NT = TT // 128
        C, Sg = self.rope_tables(s)
        q = self.al(4 * TT * 2)
        qv = q[:, :].rearrange("p (c t) -> p c t", c=4)
        kt = self.al(TT * 2)
        vt = self.al(NT * 2 * 65 * 2)
        vv = vt[:, :].rearrange("p (t g e) -> p t g e", t=NT, g=2)
        ob = [self.al(512 * 2) for _ in range(2)]
        esink = self.al(8 * 4, F32)
        QS, KS, VS = 2840, 3352, 3480
        stop = self.cfg.get('swa_stop', 9)
        if stop < 1:
            return
        self.act(esink[:, :], self.spv(l, 145, 8), AF.Exp, self.spar.res, esink.res)
        if stop < 1.2:
            return
        self.v('dve', 'memset', (vv[:, :, :, 64:65], 1.0), [], vt.res)
        if stop < 1.4:
            return
        for i in range(4):
            wt, wtv = self.wtile(W, [(0, QS + i * 64, 64), (64, QS + (i + 4) * 64, 64)])
            for tc in range(NCH):
                pa = self.pb[self.rr('pa', 2)]
                self.proj_fm(wt, wtv, hv, hres, tc, pa)
                if stop < 1.6:
                    self.act(qv[:, i, tc * 512:(tc + 1) * 512], pa[:, :], AF.Copy, pa.res, q.res)
                    continue
                self.rope(pa, qv[:, i, tc * 512:(tc + 1) * 512], q.res, C, Sg, slice(tc * 512, (tc + 1) * 512))
        if stop < 2:
            return
        wt, wtv = self.wtile(W, [(0, KS, 128)])
        for tc in range(NCH):
            pa = self.pb[self.rr('pa', 2)]
            self.proj_fm(wt, wtv, hv, hres, tc, pa)
            self.rope(pa, kt[:, tc * 512:(tc + 1) * 512], kt.res, C, Sg, slice(tc * 512, (tc + 1) * 512))
        if stop < 3:
            return
        wt, wtv = self.wtile(W, [(0, VS, 128)])
        for t in range(NT):
            pa = self.pb[self.rr('pa', 2)]
            self.tok_proj(wt, wtv, hv, hres, t, pa, 128)
            self.act(vv[:, t, :, 0:64], pa[:, 0:128].rearrange("p (g e) -> p g e", g=2), AF.Copy, pa.res, vt.res)
        if stop < 4:
            return
        diag = (self.masks[:, 0:128], self.masks.res)
        edge = (self.masks[:, 128:256], self.masks.res)
        for i in range(NT):
            o_b = ob[self.rr('ob', 2)]
            for g in range(2):
                blocks = []
                for j in (i - 1, i):
                    if j < 0:
                        continue
                    blocks.append((j, vv[:, j, g, :], vt.res, [diag] if j == i else [edge]))

                def fin(po, g=g, o_b=o_b):
                    for hh in range(4):
                        h = g * 4 + hh
                        k = self.rr('sm', 256)
                        den = self.sm[:, k:k + 1]
                        self.v('dve', 'tensor_tensor', (den, po[hh][:, 64:65], esink[:, h:h + 1], ALU.add),
                               po[hh].res + esink.res, self.sm.res)
                        self.v('dve', 'reciprocal', (den, den), self.sm.res, self.sm.res)
                        self.v('dve', 'tensor_scalar', (o_b[:, h * 64:(h + 1) * 64], po[hh][:, 0:64], den, None, ALU.mult),
                               po[hh].res + self.sm.res, o_b.res)
                self.attn_tile(qv, q.res, kt, kt.res, i, g, blocks, 0.125, 65, fin)
            self.o_to_y(o_b, yout, yres, i)

    def mix_nsa(self, l, s, W, hv, hres, yout, yres):
        TT = self.TT
        NCH = TT // 512
        NT = TT // 128
        NC_ = (TT - 32) // 16 + 1
        kcmp = self.al(128 * 2)
        vcmp = self.al(2 * 97 * 2)
        vcv = vcmp[:, :].rearrange("p (g e) -> p g e", g=2)
        KV0 = 2048
        m1 = self.mark()
        C, Sg = self.rope_tables(s)
        kc = [self.al(TT * 2) for g in range(2)]
        vc = [self.al(TT * 2) for g in range(2)]
        W1 = self.al(32 * 256 * 2)
        W1v = W1[:, :].rearrange("p (l n) -> p l n", l=32)
        w2k = self.al(2 * 2 * 128 * 2)
        w2kv = w2k[:, :].rearrange("p (h g n) -> p h g n", h=2, g=2)
        w2v = self.al(2 * 64 * 2)
        w2vv = w2v[:, :].rearrange("p (h n) -> p h n", h=2)
        posT = self.al(32 * 2)
        hid = self.al(2 * 2 * 128 * 2)
        hidv = hid[:, :].rearrange("p (g h n) -> p g h n", g=2, h=2)
        xh = self.al(128 * 4, F32)
        uu = self.al(128 * 4, F32)
        b1t = self.al(2 * 4, F32)
        for g in range(2):
            wt, wtv = self.wtile(W, [(0, KV0 + g * 64, 64)])
            for tc in range(NCH):
                pa = self.pb[self.rr('pa', 2)]
                self.proj_fm(wt, wtv, hv, hres, tc, pa, 0, 64)
                self.rope(pa, kc[g][0:64, tc * 512:(tc + 1) * 512], kc[g].res, C, Sg, slice(tc * 512, (tc + 1) * 512), P=64)
            wt, wtv = self.wtile(W, [(0, KV0 + 128 + g * 64, 64)])
            for tc in range(NCH):
                pa = self.pb[self.rr('pa', 2)]
                self.proj_fm(wt, wtv, hv, hres, tc, pa, 0, 64)
                self.act(vc[g][0:64, tc * 512:(tc + 1) * 512], pa[0:64, :], AF.Copy, pa.res, vc[g].res)
        self.v('dve', 'memset', (kcmp[:, :], 0.0), [], kcmp.res)
        self.v('dve', 'memset', (vcmp[:, :], 0.0), [], vcmp.res)
        self.v('dve', 'memset', (w2k[:, :], 0.0), [], w2k.res)
        for kvi, (nm, src) in enumerate((('k', kc), ('v', vc))):
            self.wload(W1, W1v[0:64, :, :], self.w[f'cmp_{nm}_w1'][l].rearrange("(l p) n -> p l n", p=64))
            self.wload(posT, posT[0:64, :], self.w['cmp_posT'][l, kvi])
            w2_d = self.w[f'cmp_{nm}_w2'][l].rearrange("(h p) n -> p h n", p=128)
            if nm == 'k':
                for g in range(2):
                    self.wload(w2k, w2kv[:, :, g, g * 64:(g + 1) * 64], w2_d)
            else:
                self.wload(w2v, w2vv, w2_d)
            for hc in range(2):
                ps = self.pb[6]
                for ll in range(32):
                    self.mm(ps[:, 0:1], W1v[0:64, ll, hc * 128:(hc + 1) * 128], posT[0:64, ll:ll + 1], ll == 0, ll == 31,
                            W1.res + posT.res, ps.res)
                self.v('dve', 'tensor_tensor', (b1t[:, hc:hc + 1], ps[:, 0:1], self.spv(l, 140 + 2 * kvi + hc), ALU.add),
                       ps.res + self.spar.res, b1t.res)
            for g in range(2):
                for hc in range(2):
                    ps = self.pb[self.rr('pa', 2)]
                    for ll in range(32):
                        self.mm(ps[:, 0:NC_], W1v[0:64, ll, hc * 128:(hc + 1) * 128], src[g][0:64, ll:ll + 16 * (NC_ - 1) + 1:16],
                                ll == 0, ll == 31, W1.res + src[g].res, ps.res)
                    self.v('dve', 'tensor_scalar', (xh[:, 0:NC_], ps[:, 0:NC_], b1t[:, hc:hc + 1], None, ALU.add), ps.res + b1t.res, xh.res)
                    self.v('dve', 'tensor_tensor', (uu[:, 0:NC_], xh[:, 0:NC_], xh[:, 0:NC_], ALU.mult), xh.res, uu.res)
                    self.v('dve', 'tensor_scalar', (uu[:, 0:NC_], uu[:, 0:NC_], 0.044715, 1.0, ALU.mult, ALU.add), uu.res, uu.res)
                    self.v('dve', 'tensor_tensor', (uu[:, 0:NC_], uu[:, 0:NC_], xh[:, 0:NC_], ALU.mult), uu.res + xh.res, uu.res)
                    self.act(uu[:, 0:NC_], uu[:, 0:NC_], AF.Sigmoid, uu.res, uu.res, scale=2.0 * math.sqrt(2.0 / math.pi))
                    self.v('dve', 'tensor_tensor', (hidv[:, g, hc, 0:NC_], uu[:, 0:NC_], xh[:, 0:NC_], ALU.mult), uu.res + xh.res, hid.res)
            if nm == 'k':
                ps = self.pb[self.rr('pa', 2)]
                n = 0
                for g in range(2):
                    for hc in range(2):
                        self.mm(ps[:, 0:NC_], w2kv[:, hc, g, :], hidv[:, g, hc, 0:NC_], n == 0, n == 3, w2k.res + hid.res, ps.res)
                        n += 1
                self.v('dve', 'tensor_scalar', (kcmp[:, 0:NC_], ps[:, 0:NC_], self.spv(l, 144), None, ALU.add),
                       ps.res + self.spar.res, kcmp.res)
            else:
                for g in range(2):
                    ps = self.pb[self.rr('pa', 2)]
                    for hc in range(2):
                        self.mm(ps[0:NC_, 0:64], hidv[:, g, hc, 0:NC_], w2vv[:, hc, :], hc == 0, hc == 1, w2v.res + hid.res, ps.res)
                    self.v('dve', 'tensor_tensor', (vcv[0:NC_, g, 0:64], ps[0:NC_, 0:64], self.spar[0:NC_, l * SPW + 153:l * SPW + 217], ALU.add),
                           ps.res + self.spar.res, vcmp.res)
                    self.v('dve', 'memset', (vcv[:, g, 64:65], 1.0), vcmp.res, vcmp.res)
                    self.v('dve', 'tensor_copy', (vcv[:, g, 65:97], self.ovl[:, :]), vcmp.res + self.ovl.res, vcmp.res)
        self.release(m1, True)
        C, Sg = self.rope_tables(s)
        q = self.al(4 * TT * 2)
        qv = q[:, :].rearrange("p (c t) -> p c t", c=4)
        ks = self.al(TT * 2)
        kw = self.al(TT * 2)
        vs = self.al(NT * 2 * 65 * 2)
        vw = self.al(NT * 2 * 65 * 2)
        vsv = vs[:, :].rearrange("p (t g e) -> p t g e", t=NT, g=2)
        vwv = vw[:, :].rearrange("p (t g e) -> p t g e", t=NT, g=2)
        gate = self.al(NT * 24 * 4, F32)
        gv = gate[:, :].rearrange("p (t n) -> p t n", t=NT)
        m2 = self.mark()
        wtok = self.al(DC * 280 * 2)
        QS = 1536
        for i in range(4):
            wt, wtv = self.wtile(W, [(0, QS + i * 64, 64), (64, QS + (i + 4) * 64, 64)])
            for tc in range(NCH):
                pa = self.pb[self.rr('pa', 2)]
                self.proj_fm(wt, wtv, hv, hres, tc, pa)
                self.rope(pa, qv[:, i, tc * 512:(tc + 1) * 512], q.res, C, Sg, slice(tc * 512, (tc + 1) * 512))
        for (dst, c0) in ((ks, KV0 + 256), (kw, KV0 + 512)):
            wt, wtv = self.wtile(W, [(0, c0, 128)])
            for tc in range(NCH):
                pa = self.pb[self.rr('pa', 2)]
                self.proj_fm(wt, wtv, hv, hres, tc, pa)
                self.rope(pa, dst[:, tc * 512:(tc + 1) * 512], dst.res, C, Sg, slice(tc * 512, (tc + 1) * 512))
        wtv = wtok[:, :].rearrange("p (c n) -> p c n", c=DC)
        self.wload(wtok, wtv[:, :, 0:128], W[:, :, KV0 + 256 + 128:KV0 + 256 + 256])
        self.wload(wtok, wtv[:, :, 128:256], W[:, :, KV0 + 512 + 128:KV0 + 512 + 256])
        self.wload(wtok, wtv[:, :, 256:280], W[:, :, 2816:2840])
        self.v('dve', 'memset', (vsv[:, :, :, 64:65], 1.0), [], vs.res)
        self.v('dve', 'memset', (vwv[:, :, :, 64:65], 1.0), [], vw.res)
        for t in range(NT):
            pa = self.pb[self.rr('pa', 2)]
            self.tok_proj(wtok, wtv, hv, hres, t, pa, 280)
            self.act(vsv[:, t, :, 0:64], pa[:, 0:128].rearrange("p (g e) -> p g e", g=2), AF.Copy, pa.res, vs.res)
            self.act(vwv[:, t, :, 0:64], pa[:, 128:256].rearrange("p (g e) -> p g e", g=2), AF.Copy, pa.res, vw.res)
            self.act(gv[:, t, :], pa[:, 256:280], AF.Sigmoid, pa.res, gate.res)
        self.release(m2, True)
        Eg = self.al(2 * TT * 2)
        Egv = Eg[:, :].rearrange("p (g k) -> p g k", g=2)
        ob = [self.al(512 * 2) for _ in range(2)]
        of = [self.al(512 * 4, F32) for _ in range(2)]
        misc = self.al(2048, F32)

        class _V:
            def __init__(s_, ap, res):
                s_.ap = ap
                s_.res = res

            def __getitem__(s_, k):
                return s_.ap[k]
        imp = _V(misc[:, 0:64], misc.res)
        sc1 = _V(misc[:, 64:128], misc.res)
        sc2 = _V(misc[:, 128:192], misc.res)
        m8 = _V(misc[:, 192:208], misc.res)
        selb = _V(misc[:, 256:288].bitcast(BF), misc.res)
        selT = [self.al(128 * 2) for _ in range(2)]
        self.dma('sp', Eg[0:64, :], self.Eg_d[:, :], [], Eg.res)
        cmask = self.al(16 * 128 * 2)
        tkb = self.al(NT * 64 * 4, F32)
        self.dma('sp', cmask[:, :], self.masks_d[:, 256:18 * 128], [], cmask.res)
        self.dma('sp', tkb[:, :], self.tkb_d[:, :], [], tkb.res)
        diag = (self.masks[:, 0:128], self.masks.res)
        edge = (self.masks[:, 128:256], self.masks.res)
        for i in range(NT):
            o_f = of[self.rr('of', 2)]
            o_b = ob[self.rr('ob', 2)]

            def fin_branch(po, g, bidx, first, want_imp=False, o_f=o_f, i=i):
                for hh in range(4):
                    h = g * 4 + hh
                    k = self.rr('sm', 128) * 2
                    den = self.sm[:, k:k + 1]
                    scl = self.sm[:, k + 1:k + 2]
                    self.v('dve', 'tensor_scalar', (den, po[hh][:, 64:65], 1e-30, None, ALU.max), po[hh].res, self.sm.res)
                    self.v('dve', 'reciprocal', (den, den), self.sm.res, self.sm.res)
                    self.v('dve', 'tensor_tensor', (scl, den, gv[:, i, 3 * h + bidx:3 * h + bidx + 1], ALU.mult),
                           self.sm.res + gate.res, self.sm.res)
                    osl = o_f[:, h * 64:(h + 1) * 64]
                    if first:
                        self.v('dve', 'tensor_scalar', (osl, po[hh][:, 0:64], scl, None, ALU.mult), po[hh].res + self.sm.res, o_f.res)
                    else:
                        self.v('dve', 'scalar_tensor_tensor', (osl, po[hh][:, 0:64], scl, osl, ALU.mult, ALU.add),
                               po[hh].res + self.sm.res + o_f.res, o_f.res)
                    if want_imp:
                        isl = imp[:, g * 32:(g + 1) * 32]
                        if hh == 0:
                            self.v('dve', 'tensor_scalar', (isl, po[hh][:, 65:97], den, None, ALU.mult), po[hh].res + self.sm.res, imp.res)
                        else:
                            self.v('dve', 'scalar_tensor_tensor', (isl, po[hh][:, 65:97], den, isl, ALU.mult, ALU.add),
                                   po[hh].res + self.sm.res + imp.res, imp.res)
            cm = (cmask[:, i * 128:(i + 1) * 128], cmask.res)
            for g in range(2):
                self.attn_tile(qv, q.res, kcmp, kcmp.res, i, g, [(0, vcv[:, g, :], vcmp.res, [cm])], 0.125, 97,
                               lambda po, g=g: fin_branch(po, g, 0, True, True))
            dyn = (2 * i + 2) > 16
            st = None
            if dyn:
                st = selT[self.rr('selT', 2)]
                self.v('dve', 'tensor_tensor', (sc1[:, :], imp[:, :], tkb[:, i * 64:(i + 1) * 64], ALU.add), imp.res + tkb.res, sc1.res)
                for g in range(2):
                    gs = slice(g * 32, (g + 1) * 32)
                    self.v('dve', 'max', (), sc1.res, m8.res, out=m8[:, 0:8], in_=sc1[:, gs])
                    self.v('dve', 'match_replace', (), m8.res + sc1.res, sc2.res, out=sc2[:, gs], in_to_replace=m8[:, 0:8],
                           in_values=sc1[:, gs], imm_value=-3.0e38)
                    self.v('dve', 'max', (), sc2.res, m8.res, out=m8[:, 8:16], in_=sc2[:, gs])
                    self.v('dve', 'tensor_reduce', (m8[:, 0:1], m8[:, 8:16], AX.X, ALU.min), m8.res, m8.res)
                    self.v('dve', 'tensor_scalar', (selb[:, gs], sc1[:, gs], m8[:, 0:1], None, ALU.is_ge), sc1.res + m8.res, selb.res)
                pT = self.pbT
                self.tr(pT[0:64, 0:128], selb[:, 0:64], self.ident, selb.res + self.cst_bf.res, pT.res)
                self.act(st[0:64, :], pT[0:64, 0:128], AF.Copy, pT.res, st.res)
                if self.cfg.get('debug'):
                    self.dma('sp', self.dbg2_d[:, i * 128:(i + 1) * 128], st[0:64, :], st.res, [])
                    self.dma('sp', self.dbg3_d[:, i * 64:(i + 1) * 64], sc1[:, :], sc1.res, [])
            for g in range(2):
                blocks = []
                for j in range(i + 1):
                    masks = []
                    if dyn:
                        def mkmask(g=g, j=j, st=st):
                            pr = self.rr('pm', 2)
                            pm = self.pb[6]
                            self.mm(pm[:, pr * 128:(pr + 1) * 128], Egv[0:64, g, j * 128:(j + 1) * 128], st[0:64, :], True, True,
                                    Eg.res + st.res, pm.res)
                            return (pm[:, pr * 128:(pr + 1) * 128], pm.res)
                        masks.append(mkmask)
                    if j == i:
                        masks.append(diag)
                    blocks.append((j, vsv[:, j, g, :], vs.res, masks))
                self.attn_tile(qv, q.res, ks, ks.res, i, g, blocks, 0.125, 65, lambda po, g=g: fin_branch(po, g, 1, False))
            for g in range(2):
                blocks = []
                for j in range(max(0, i - 4), i + 1):
                    masks = [diag] if j == i else ([edge] if j == i - 4 else [])
                    blocks.append((j, vwv[:, j, g, :], vw.res, masks))
                self.attn_tile(qv, q.res, kw, kw.res, i, g, blocks, 0.125, 65, lambda po, g=g: fin_branch(po, g, 2, False))
            self.act(o_b[:, :], o_f[:, :], AF.Copy, o_f.res, o_b.res)
            self.o_to_y(o_b, yout, yres, i)

    def mix_merge(self, l, hT, hv, yv, yres):
        TT = self.TT
        NCH = TT // 512
        W = self.w['w_in'][l].rearrange("(kc p) n -> p kc n", p=128)
        Wb = self.w['w_branch'][l].rearrange("n (kc p) d -> p (n kc) d", p=128)
        Wo = self.w['w_out'][l].rearrange("(kc p) n -> p kc n", p=128)
        merged = self.al(DC * TT * 2)
        mv = merged[:, :].rearrange("p (c t) -> p c t", c=DC)
        m0 = self.mark()
        wg = [self.al(DC * 512 * 2) for _ in range(2)]
        wb = [self.al(16 * 128 * 2) for _ in range(2)]
        tf = [self.al(2048, F32) for _ in range(4)]
        GS = 3608
        for d in range(DC):
            wgt = wg[d % 2]
            wgv = wgt[:, :].rearrange("p (c n) -> p c n", c=DC)
            for n in range(4):
                self.wload(wgt, wgv[:, :, n * 128:(n + 1) * 128], W[:, :, GS + n * 1024 + d * 128:GS + n * 1024 + (d + 1) * 128])
            wbt = wb[d % 2]
            wbv = wbt[:, :].rearrange("p (k n) -> p k n", k=16)
            self.wload(wbt, wbv, Wb[:, :, d * 128:(d + 1) * 128])
            for tc in range(NCH):
                sl = slice(tc * 512, (tc + 1) * 512)
                acc = tf[self.rr('tfa', 2)]
                for n in range(4):
                    pg = self.pb[self.rr('pa', 2)]
                    pk = self.pb[2 + self.rr('pb', 2)]
                    for c in range(DC):
                        self.mm(pg[:, :], wgv[:, c, n * 128:(n + 1) * 128], hv[:, c, sl], c == 0, c == DC - 1, wgt.res + hT.res, pg.res)
                    for kc in range(4):
                        self.mm(pk[:, :], wbv[:, n * 4 + kc, :], yv[n][:, kc, sl], kc == 0, kc == 3, wbt.res + yres, pk.res)
                    sg = tf[2 + self.rr('tfs', 2)]
                    self.act(sg[:, :], pg[:, :], AF.Sigmoid, pg.res, sg.res)
                    if n == 0:
                        self.v('dve', 'tensor_tensor', (acc[:, :], sg[:, :], pk[:, :], ALU.mult), sg.res + pk.res, acc.res)
                    else:
                        self.v('dve', 'tensor_tensor', (sg[:, :], sg[:, :], pk[:, :], ALU.mult), sg.res + pk.res, sg.res)
                        dst = mv[:, d, sl] if n == 3 else acc[:, :]
                        dres = merged.res if n == 3 else acc.res
                        self.v('dve', 'tensor_tensor', (dst, acc[:, :], sg[:, :], ALU.add), acc.res + sg.res, dres)
        self.release(m0)
        self.dma('sp', self.xT[:, :, :], self.xs_d[:, :, :], [self.xs_res], self.xT.res)
        wo = self.al(DC * D * 2)
        wov = wo[:, :].rearrange("p (c n) -> p c n", c=DC)
        self.wload(wo, wov, Wo)
        yv_ = hT[:, 0:DC * 512 * 2].bitcast(F32).rearrange("p (c t) -> p c t", c=DC)
        sqv = hT[:, DC * 512 * 2:DC * 512 * 3].rearrange("p (c t) -> p c t", c=DC)
        for tc in range(NCH):
            sl = slice(tc * 512, (tc + 1) * 512)
            for d in range(DC):
                py = self.pb[4 + self.rr('py', 2)]
                for c in range(DC):
                    self.mm(py[:, :], wov[:, c, d * 128:(d + 1) * 128], mv[:, c, sl], c == 0, c == DC - 1, wo.res + merged.res, py.res)
                self.act(yv_[:, d, :], py[:, :], AF.Copy, py.res, hT.res)
                self.act(sqv[:, d, :], py[:, :], AF.Square, py.res, hT.res)
            self.rms_post_add(yv_, hT.res, sqv, hT.res, self.gain(l, 3), 1.0, tc)

    def alloc_small(self):
        self.rstd = [self.al(2048, F32) for i in range(2)]
        self.tmpf = [self.al(2048, F32) for i in range(2)]
        self.tmpb = [self.al(1024) for i in range(4)]
        self.sm = self.al(2048, F32)


GAIN_NAMES = ['ffn1_pre_g', 'ffn1_post_g', 'mix_pre_g', 'mix_post_g', 'x_pre_g', 'x_post_g', 'ffn2_pre_g', 'ffn2_post_g', 'mem_g']


def host_consts(inputs, L, TT=2048):
    bf = ml_dtypes.bfloat16
    NT = TT // 128
    p = np.arange(128)
    cst = np.zeros((128, 512), np.float32)
    cst[:, 0:128] = np.eye(128, dtype=np.float32)
    cst[:, 128:256] = 1.0 / D
    cst[:, 256:384] = 1.0 / 512
    partner = np.where((p % 64) < 32, p + 32, p - 32)
    Rm = np.zeros((128, 128), np.float32)
    Rm[partner, p] = 1.0
    cst[:, 384:512] = Rm
    gains = np.concatenate(
        [np.asarray(inputs[n][l], np.float32).reshape(DC, 128).T for l in range(L) for n in GAIN_NAMES], axis=1)
    kl = p[:, None]
    ql = p[None, :]
    masks = [(kl <= ql), (ql < kl)]
    for i in range(16):
        masks.append((16 * kl + 31 <= 128 * i + ql) & (kl < 127))
    masks = np.concatenate([m.astype(np.float32) for m in masks], axis=1)
    tkb = np.zeros((128, NT, 2, 32), np.float32)
    for i in range(NT):
        qpos = 128 * i + p
        j = np.arange(32)
        causal = (64 * j[None, :] <= qpos[:, None])
        forced = (j[None, :] == 0) | (j[None, :] == (qpos // 64)[:, None])
        val = np.where(causal, np.where(forced, 1e4, 0.0), -1e30).astype(np.float32)
        tkb[:, i, 0, :] = val
        tkb[:, i, 1, :] = val
    gc = np.zeros((128, 66), np.float32)
    inv = 10000.0 ** (-(2.0 * (p % 32)) / 64.0)
    gc[:, 0] = inv / (2 * np.pi)
    gc[:, 1] = 2 * np.pi * np.where((p % 64) < 32, -1.0, 1.0)
    for c in range(4):
        gc[:, 2 + c * 16:2 + (c + 1) * 16] = 1.0 / np.minimum(np.arange(16) + 1, 2 << c)
    cs = np.arange(128) * 16
    js = np.arange(32) * 64
    ovl = ((cs[:, None] <= js[None, :] + 63) & (cs[:, None] + 31 >= js[None, :]) & (np.arange(128)[:, None] < 127))
    Eg = np.zeros((64, 2, TT), np.float32)
    keys = np.arange(TT)
    for g in range(2):
        Eg[32 * g + keys // 64, g, keys] = 1.0
    spar = []
    for l in range(L):
        f = lambda n: np.asarray(inputs[n][l], np.float32)
        cols = [f('conv_w').T.reshape(4, 128, 31).transpose(1, 0, 2).reshape(128, 124)]
        for n in ('conv_b', 'conv_ln_g', 'conv_ln_b', 'pool_scale'):
            cols.append(f(n).reshape(4, 128).T)
        cols.append(f('cmp_k_b1').reshape(2, 128).T)
        cols.append(f('cmp_v_b1').reshape(2, 128).T)
        cols.append(np.tile(f('cmp_k_b2'), 2)[:, None])
        cols.append(np.broadcast_to(f('swa_sinks')[None, :], (128, 8)))
        cols.append(np.broadcast_to(f('cmp_v_b2')[None, :], (128, 64)))
        spar.append(np.concatenate(cols, axis=1))
    spar = np.concatenate(spar, axis=1)
    assert spar.shape[1] == L * SPW
    posT = np.stack([np.stack([np.asarray(inputs['cmp_k_pos'][l], np.float32).T,
                               np.asarray(inputs['cmp_v_pos'][l], np.float32).T]) for l in range(L)])
    return {
        'cst_bf': cst.astype(bf),
        'ident_f': np.eye(128, dtype=np.float32),
        'gains': np.ascontiguousarray(gains),
        'masks': masks.astype(bf),
        'tkb': np.ascontiguousarray(tkb.reshape(128, NT * 64)),
        'gcst': gc,
        'spar': np.ascontiguousarray(spar),
        'ovl': ovl.astype(np.float32).astype(bf),
        'Eg': Eg.reshape(64, 2 * TT).astype(bf),
        'cmp_posT': np.ascontiguousarray(posT),
    }


_CACHE = {}


WNAMES = ['ffn1_w_in', 'ffn1_w_out', 'ffn2_w_in', 'ffn2_w_out', 'w_xq', 'w_xkv', 'w_xo', 'w_in', 'pool_w',
          'cmp_k_w1', 'cmp_v_w1', 'cmp_k_w2', 'cmp_v_w2', 'w_branch', 'w_out']


def kernel(**inputs):
    NCORE = 8
    B = inputs['x'].shape[0]
    L = inputs['ffn1_w_in'].shape[0]
    nseq = B // NCORE
    cfg = {'nseq': nseq, 'layers': L, 'T': inputs['x'].shape[1]}
    b = Builder(cfg)
    nc = b.build()
    consts = host_consts(inputs, L)
    in_maps = []
    for c in range(NCORE):
        m = dict(consts)
        m['x'] = np.ascontiguousarray(inputs['x'][c * nseq:(c + 1) * nseq])
        m['mem'] = np.ascontiguousarray(inputs['mem'][c * nseq:(c + 1) * nseq])
        m['positions'] = np.ascontiguousarray(inputs['positions'][c * nseq:(c + 1) * nseq]).astype(np.int32)
        for k in WNAMES:
            m[k] = np.asarray(inputs[k])
        in_maps.append(m)
    res = run_bass_kernel_spmd(nc, in_maps, core_ids=list(range(NCORE)))
    return np.concatenate([r['out'] for r in res.results], axis=0)
```

```python
import math
import numpy as np
import ml_dtypes
from contextlib import ExitStack
import concourse.bass as bass
import concourse.mybir as mybir
from concourse.bass_utils import run_bass_kernel_spmd

F32 = mybir.dt.float32
BF = mybir.dt.bfloat16
I32 = mybir.dt.int32
AF = mybir.ActivationFunctionType
ALU = mybir.AluOpType
AX = mybir.AxisListType

D = 1024
DC = 8
DFF = 2816
FC = 22
EPS = 1e-6
NG = 9
SPW = 217

ENGS = ['pe', 'act', 'dve', 'pool', 'sp']
SAME_ENGINE_SYNC = {'act', 'dve', 'pool'}
EPOCH = 20000
NSLOT = 8


class Res:
    __slots__ = ('w', 'rs', 'name', 'excl')

    def __init__(self, name='', excl=False):
        self.w = None
        self.rs = {}
        self.name = name
        self.excl = excl


class Op:
    __slots__ = ('eng', 'fn', 'deps', 'idx', 'tick', 'dma', 'sig', 'slot', 'sval', 'slotwait')

    def __init__(self, eng, fn, dma):
        self.eng = eng
        self.fn = fn
        self.dma = dma
        self.deps = {}
        self.sig = False
        self.tick = None
        self.slot = None
        self.sval = None
        self.slotwait = None


class Sched:
    def __init__(self):
        self.ops = {e: [] for e in ENGS}

    def add(self, eng, fn, reads=(), writes=(), dma=False):
        op = Op(eng, fn, dma)
        op.idx = len(self.ops[eng])
        self.ops[eng].append(op)
        deps = op.deps

        def dep(o):
            if o is op:
                return
            if o.dma:
                deps[('dma', id(o))] = o
            else:
                k = o.eng
                if k not in deps or deps[k].idx < o.idx:
                    deps[k] = o
        for r in reads:
            if r.w is not None:
                dep(r.w)
            if r.excl:
                for o in r.rs.values():
                    if o.dma or o.eng != eng:
                        dep(o)
        for w in writes:
            if w.w is not None:
                dep(w.w)
            for o in w.rs.values():
                dep(o)
        key = ('dma', id(op)) if dma else eng
        for r in reads:
            r.rs[key] = op
        for w in writes:
            w.w = op
            w.rs = {}
        return op

    def emit(self, nc):
        for e in ENGS:
            for op in self.ops[e]:
                for k, d in list(op.deps.items()):
                    if (not d.dma) and d.eng == op.eng and (not op.dma) and op.eng not in SAME_ENGINE_SYNC:
                        del op.deps[k]
                        continue
                    d.sig = True
        nsem = {}
        for e in ENGS:
            t = 0
            nd = 0
            for op in self.ops[e]:
                if op.dma:
                    op.slot = nd % NSLOT
                    op.sval = 16 * (nd // NSLOT + 1)
                    op.slotwait = 16 * (nd // NSLOT)
                    nd += 1
                elif op.sig:
                    t += 1
                    op.tick = t
            nsem[e] = (t // EPOCH + 1, nd > 0)
        with ExitStack() as es:
            csem = {}
            dsem = {}
            for e in ENGS:
                csem[e] = [es.enter_context(nc.semaphore(f"c_{e}_{i}")) for i in range(nsem[e][0])]
                if nsem[e][1]:
                    dsem[e] = [es.enter_context(nc.semaphore(f"d_{e}_{i}")) for i in range(NSLOT)]
            block = es.enter_context(nc.Block())
            handles = {'pe': 'tensor', 'act': 'scalar', 'dve': 'vector', 'pool': 'gpsimd', 'sp': 'sync'}

            def emit_engine(e, h):
                known = {}
                for op in self.ops[e]:
                    for k, d in op.deps.items():
                        if d.dma:
                            sem = dsem[d.eng][d.slot]
                            val = d.sval
                            kk = ('d', d.eng, d.slot)
                        else:
                            ep = (d.tick - 1) // EPOCH
                            sem = csem[d.eng][ep]
                            val = d.tick - ep * EPOCH
                            kk = ('c', d.eng, ep)
                        if known.get(kk, 0) >= val:
                            continue
                        known[kk] = val
                        h.wait_ge(sem, val)
                    if op.dma:
                        if op.slotwait > 0:
                            kk = ('d', e, op.slot)
                            if known.get(kk, 0) < op.slotwait:
                                known[kk] = op.slotwait
                                h.wait_ge(dsem[e][op.slot], op.slotwait)
                        ins = op.fn(h)
                        ins.then_inc(dsem[e][op.slot], 16)
                    else:
                        ins = op.fn(h)
                        if op.sig:
                            ep = (op.tick - 1) // EPOCH
                            ins.then_inc(csem[e][ep], 1)
                if e in dsem:
                    last = {}
                    for op in self.ops[e]:
                        if op.dma:
                            last[op.slot] = op.sval
                    for s, v in last.items():
                        h.wait_ge(dsem[e][s], v)

            for e in ENGS:
                if not self.ops[e]:
                    continue
                getattr(block, handles[e])(lambda h, e=e: emit_engine(e, h))


class T:
    def __init__(self, h, nres=1, name=''):
        self.h = h
        self.res = [Res(f"{name}{i}") for i in range(nres)]

    def __getitem__(self, k):
        return self.h[k]


class Ring:
    def __init__(self, tiles):
        self.tiles = tiles
        self.i = 0

    def next(self):
        t = self.tiles[self.i % len(self.tiles)]
        self.i += 1
        return t


class Builder:
    def __init__(self, cfg):
        self.cfg = cfg
        self.S = Sched()
        self.nc = bass.Bass("TRN2", target_bir_lowering=False)
        self.es = ExitStack()

    def sb(self, name, shape, dt, nres=1):
        h = self.es.enter_context(self.nc.sbuf_tensor('s_' + name, list(shape), dt))
        return T(h, nres, name)

    def ps(self, name, shape, dt=F32):
        h = self.es.enter_context(self.nc.psum_tensor('p_' + name, list(shape), dt))
        t = T(h, 1, name)
        t.res[0].excl = True
        return t

    def din(self, name, shape, dt):
        return self.nc.dram_tensor(name, list(shape), dt, kind="ExternalInput").ap()

    def dout(self, name, shape, dt):
        return self.nc.dram_tensor(name, list(shape), dt, kind="ExternalOutput").ap()

    def ar_reset(self):
        self.ar_off = 0
        self.ar_top = self.AR_BYTES

    def al_top(self, nbytes, dt=BF):
        n = (nbytes + 2047) // 2048 * 2048
        off = self.ar_top - n
        assert off >= self.ar_off, (off, self.ar_off)
        self.ar_top = off
        ap = self.arena.h[:, off // 2:(off + nbytes) // 2]
        if dt != BF:
            ap = ap.bitcast(dt)
        t = T(ap, 0)
        t.res = self.arena.res[off // 2048:(off + n) // 2048]
        return t

    def al(self, nbytes, dt=BF):
        off = self.ar_off
        n = (nbytes + 2047) // 2048 * 2048
        assert off + n <= self.ar_top, (off, n, self.ar_top)
        self.ar_off = off + n
        ap = self.arena.h[:, off // 2:(off + nbytes) // 2]
        if dt != BF:
            ap = ap.bitcast(dt)
        t = T(ap, 0)
        t.res = self.arena.res[off // 2048:(off + n) // 2048]
        return t

    def mm(self, out, lhsT, rhs, start, stop, reads, writes, skip=False):
        if skip:
            self.S.add('pe', lambda e: e.matmul(out, lhsT, rhs, start=start, stop=stop, skip_group_check=True), reads, writes)
        else:
            self.S.add('pe', lambda e: e.matmul(out, lhsT, rhs, start=start, stop=stop), reads, writes)

    def tr(self, out, in_, ident, reads, writes):
        self.S.add('pe', lambda e: e.transpose(out, in_, ident), reads, writes)

    def act(self, out, in_, func, reads, writes, bias=None, scale=None):
        kw = {}
        if bias is not None:
            kw['bias'] = bias
        if scale is not None:
            kw['scale'] = scale
        self.S.add('act', lambda e: e.activation(out, in_, func, **kw), reads, writes)

    def dma(self, eng, out, in_, reads, writes):
        self.S.add(eng, lambda e: e.dma_start(out=out, in_=in_), reads, writes, dma=True)

    def v(self, eng, name, args, reads, writes, **kw):
        self.S.add(eng, lambda e: getattr(e, name)(*args, **kw), reads, writes)

    def build(self):
        cfg = self.cfg
        nc = self.nc
        NSEQ = cfg['nseq']
        L = cfg['layers']
        TT = cfg.get('T', 2048)
        self.TT = TT
        NT = TT // 128
        NCH = TT // 512
        phases = cfg.get('phases', ['ffn1', 'mix', 'xatt', 'ffn2'])

        x_d = self.din('x', [NSEQ, TT, D], F32)
        out_d = self.dout('out', [NSEQ, TT, D], F32)
        gains_d = self.din('gains', [128, L * NG * DC], F32)
        cst_bf_d = self.din('cst_bf', [128, 512], BF)
        ident_f_d = self.din('ident_f', [128, 128], F32)
        w = {}
        w['ffn1_w_in'] = self.din('ffn1_w_in', [L, D, 2 * DFF], F32)
        w['ffn1_w_out'] = self.din('ffn1_w_out', [L, DFF, D], F32)
        w['ffn2_w_in'] = self.din('ffn2_w_in', [L, D, 2 * DFF], F32)
        w['ffn2_w_out'] = self.din('ffn2_w_out', [L, DFF, D], F32)
        for n_, shp in [('w_xq', [L, D, D]), ('w_xkv', [L, D, 2 * D]), ('w_xo', [L, D, D])]:
            w[n_] = self.din(n_, shp, F32)
        self.mem_d = self.din('mem', [NSEQ, 256, D], F32)
        w['w_in'] = self.din('w_in', [L, D, 7704], F32)
        w['pool_w'] = self.din('pool_w', [L, 4, 128, 128], F32)
        for nm in ('k', 'v'):
            w[f'cmp_{nm}_w1'] = self.din(f'cmp_{nm}_w1', [L, 2048, 256], F32)
            w[f'cmp_{nm}_w2'] = self.din(f'cmp_{nm}_w2', [L, 256, 64], F32)
        w['cmp_posT'] = self.din('cmp_posT', [L, 2, 64, 32], F32)
        w['w_branch'] = self.din('w_branch', [L, 4, 512, D], F32)
        w['w_out'] = self.din('w_out', [L, D, D], F32)
        self.pos_d = self.din('positions', [NSEQ, TT], I32)
        masks_d = self.din('masks', [128, 18 * 128], BF)
        tkb_d = self.din('tkb', [128, NT * 64], F32)
        gcst_d = self.din('gcst', [128, 66], F32)
        spar_d = self.din('spar', [128, L * SPW], F32)
        ovl_d = self.din('ovl', [128, 32], BF)
        self.Eg_d = self.din('Eg', [64, 2 * TT], BF)
        self.xs_d = self.nc.dram_tensor('xs', [128, DC, TT], F32, kind="Internal").ap()
        self.xs_res = Res('xs')
        if cfg.get('debug'):
            self.dbg_d = self.dout('dbg', [128, 16 * TT], BF)
            self.dbg2_d = self.dout('dbg2', [64, TT], BF)
            self.dbg3_d = self.dout('dbg3', [128, NT * 64], F32)
        self.w = w

        self.xT = self.sb('xT', [128, DC, TT], F32, nres=NCH)
        self.gains = self.sb('gains', [128, L * NG * DC], F32)
        self.cst_bf = self.sb('cst_bf', [128, 512], BF)
        self.masks = self.sb('masks', [128, 2 * 128], BF)
        self.gcst = self.sb('gcst', [128, 66], F32)
        self.spar = self.sb('spar', [128, L * SPW], F32)
        self.ovl = self.sb('ovl', [128, 32], BF)
        self.ident_f = self.sb('ident_f', [128, 128], F32)
        self.ident = self.cst_bf[:, 0:128]
        self.onesD = self.cst_bf[:, 128:256]
        self.onesC = self.cst_bf[:, 256:384]
        self.Rm = self.cst_bf[:, 384:512]
        self.AR_BYTES = 134 * 1024
        self.arena = self.sb('arena', [128, self.AR_BYTES // 2], BF, nres=self.AR_BYTES // 2048)
        self.ar_off = 0
        self.pb = [self.ps(f'pb{i}', [128, 512], F32) for i in range(7)]
        self.pbT = self.ps('pbT', [128, 1024], BF)
        self.cnt = {}

        S = self.S
        self.dma('sp', self.gains[:, :], gains_d[:, :], [], self.gains.res)
        self.dma('sp', self.cst_bf[:, :], cst_bf_d[:, :], [], self.cst_bf.res)
        self.dma('sp', self.ident_f[:, :], ident_f_d[:, :], [], self.ident_f.res)
        self.dma('sp', self.masks[:, :], masks_d[:, 0:256], [], self.masks.res)
        self.masks_d = masks_d
        self.tkb_d = tkb_d
        self.dma('sp', self.gcst[:, :], gcst_d[:, :], [], self.gcst.res)
        self.dma('sp', self.spar[:, :], spar_d[:, :], [], self.spar.res)
        self.dma('sp', self.ovl[:, :], ovl_d[:, :], [], self.ovl.res)

        for s in range(NSEQ):
            self.load_x(x_d, s)
            for l in range(L):
                for ph in phases:
                    if ph == 'ffn1':
                        self.ffn(l, 'ffn1', 0, 1)
                    elif ph == 'ffn2':
                        self.ffn(l, 'ffn2', 6, 7)
                    elif ph == 'xatt':
                        self.xatt(l, s)
                    elif ph == 'mix':
                        self.mixer(l, s)
            self.store_x(out_d, s)
        S.emit(nc)
        self.es.close()
        return nc

    def rr(self, key, n):
        i = self.cnt.get(key, 0)
        self.cnt[key] = i + 1
        return i % n

    def load_x(self, x_d, s):
        NT = self.TT // 128
        self.ar_reset()
        self.xin = [self.al(D * 4, F32) for i in range(2)]
        for t in range(NT):
            xi = self.xin[self.rr('xin', 2)]
            self.dma('sp', xi[:, :], x_d[s, t * 128:(t + 1) * 128, :], [], xi.res)
            for half in range(2):
                bank = self.pb[self.rr('pbx', 2)]
                for c4 in range(4):
                    c = half * 4 + c4
                    self.tr(bank[:, c4 * 128:(c4 + 1) * 128], xi[:, c * 128:(c + 1) * 128], self.ident_f[:, :],
                            xi.res + self.ident_f.res, bank.res)
                dst = self.xT[:, half * 4:half * 4 + 4, t * 128:(t + 1) * 128]
                src = bank[:, :].rearrange("p (c t) -> p c t", c=4)
                eng = 'act' if half == 0 else 'dve'
                if eng == 'act':
                    self.act(dst, src, AF.Copy, bank.res, [self.xT.res[t // 4]])
                else:
                    self.v('dve', 'tensor_copy', (dst, src), bank.res, [self.xT.res[t // 4]])

    def store_x(self, out_d, s):
        NT = self.TT // 128
        self.ar_reset()
        self.xin = [self.al(D * 4, F32) for i in range(2)]
        for t in range(NT):
            xi = self.xin[self.rr('xin', 2)]
            for half in range(2):
                bank = self.pb[self.rr('pbx', 2)]
                for c4 in range(4):
                    c = half * 4 + c4
                    self.tr(bank[:, c4 * 128:(c4 + 1) * 128], self.xT[:, c, t * 128:(t + 1) * 128], self.ident_f[:, :],
                            [self.xT.res[t // 4]] + self.ident_f.res, bank.res)
                dst = xi[:, half * 512:(half + 1) * 512]
                if half == 0:
                    self.act(dst, bank[:, :], AF.Copy, bank.res, xi.res)
                else:
                    self.v('dve', 'tensor_copy', (dst, bank[:, :]), bank.res, xi.res)
            self.dma('sp', out_d[s, t * 128:(t + 1) * 128, :], xi[:, :], xi.res, [])

    def rsqrt_eps(self, rstd, stat):
        self.v('dve', 'tensor_scalar', (rstd[:, :], stat[:, :], EPS, None, ALU.add), stat.res, rstd.res)
        self.act(rstd[:, :], rstd[:, :], AF.Sqrt, rstd.res, rstd.res)
        self.v('dve', 'reciprocal', (rstd[:, :], rstd[:, :]), rstd.res, rstd.res)

    def gain(self, l, gi):
        o = (l * NG + gi) * DC
        return self.gains[:, o:o + DC]

    def rms_pre(self, src, src_res, g, dst, dst_res, sq, sq_res):
        i = self.rr('rstd', 2)
        rstd = self.rstd[i]
        stat = self.pb[6]
        self.act(sq, src, AF.Square, src_res, sq_res)
        for c in range(DC):
            self.mm(stat[:, :], self.onesD, sq[:, c, :], c == 0, c == DC - 1, sq_res + self.cst_bf.res, stat.res)
        self.rsqrt_eps(rstd, stat)
        for c in range(DC):
            self.v('dve', 'scalar_tensor_tensor', (dst[:, c, :], src[:, c, :], g[:, c:c + 1], rstd[:, :], ALU.mult, ALU.mult),
                   src_res + rstd.res + self.gains.res, dst_res)

    def rms_post_add(self, y, y_res, sq, sq_res, g, alpha, tc):
        i = self.rr('rstd', 2)
        rstd = self.rstd[i]
        stat = self.pb[6]
        for c in range(DC):
            self.mm(stat[:, :], self.onesD, sq[:, c, :], c == 0, c == DC - 1, sq_res + self.cst_bf.res, stat.res)
        self.rsqrt_eps(rstd, stat)
        xr = [self.xT.res[tc]]
        for c in range(DC):
            tmp = self.tmpf[self.rr('tmpf', 2)]
            self.v('dve', 'scalar_tensor_tensor', (tmp[:, :], y[:, c, :], g[:, c:c + 1], rstd[:, :], ALU.mult, ALU.mult),
                   y_res + rstd.res + self.gains.res, tmp.res)
            xs = self.xT[:, c, tc * 512:(tc + 1) * 512]
            self.v('dve', 'scalar_tensor_tensor', (xs, tmp[:, :], float(alpha), xs, ALU.mult, ALU.add),
                   tmp.res + xr, xr)

    def ffn(self, l, name, gpre, gpost):
        w_in = self.w[name + '_w_in'][l].rearrange("(kc p) n -> p kc n", p=128)
        w_out = self.w[name + '_w_out'][l].rearrange("(kc p) n -> p kc n", p=128)
        NH = self.TT // 1024
        self.ar_reset()
        hT = [self.al(DC * 512 * 2) for j in range(2)]
        gT = [self.al(FC * 512 * 2) for j in range(2)]
        y = [self.al(DC * 512 * 4, F32) for j in range(2)]
        wra = [self.al(DC * 256 * 2) for j in range(3)]
        wrb = [self.al(FC * 128 * 2) for j in range(2)]
        self.alloc_small()
        hv = [t[:, :].rearrange("p (c t) -> p c t", c=DC) for t in hT]
        gv = [t[:, :].rearrange("p (f t) -> p f t", f=FC) for t in gT]
        yv = [t[:, :].rearrange("p (c t) -> p c t", c=DC) for t in y]
        for half in range(NH):
            for j in range(2):
                tc = half * 2 + j
                self.rms_pre(self.xT[:, :, tc * 512:(tc + 1) * 512], [self.xT.res[tc]], self.gain(l, gpre),
                             hv[j], hT[j].res, hv[j], hT[j].res)
            for f in range(FC):
                wt = wra[self.rr('wra', 3)]
                wv = wt[:, :].rearrange("p (c n) -> p c n", c=DC)
                self.dma('pool', wv[:, :, 0:128], w_in[:, :, f * 128:(f + 1) * 128], [], wt.res)
                self.dma('pool', wv[:, :, 128:256], w_in[:, :, DFF + f * 128:DFF + (f + 1) * 128], [], wt.res)
                for j in range(2):
                    pa = self.pb[self.rr('pa', 2)]
                    pbk = self.pb[2 + self.rr('pb', 2)]
                    for c in range(DC):
                        self.mm(pa[:, :], wv[:, c, 0:128], hv[j][:, c, :], c == 0, c == DC - 1,
                                wt.res + hT[j].res, pa.res)
                    for c in range(DC):
                        self.mm(pbk[:, :], wv[:, c, 128:256], hv[j][:, c, :], c == 0, c == DC - 1,
                                wt.res + hT[j].res, pbk.res)
                    sa = self.tmpb[self.rr('tmpb', 4)]
                    self.act(sa[:, :], pa[:, :], AF.Silu, pa.res, sa.res)
                    self.v('dve', 'tensor_tensor', (gv[j][:, f, :], sa[:, :], pbk[:, :], ALU.mult),
                           sa.res + pbk.res, gT[j].res)
            for d in range(DC):
                wt = wrb[self.rr('wrb', 2)]
                wv = wt[:, :].rearrange("p (f n) -> p f n", f=FC)
                self.dma('pool', wv, w_out[:, :, d * 128:(d + 1) * 128], [], wt.res)
                for j in range(2):
                    py = self.pb[4 + self.rr('py', 2)]
                    for f in range(FC):
                        self.mm(py[:, :], wv[:, f, :], gv[j][:, f, :], f == 0, f == FC - 1,
                                wt.res + gT[j].res, py.res)
                    self.act(yv[j][:, d, :], py[:, :], AF.Copy, py.res, y[j].res)
                    self.act(hv[j][:, d, :], py[:, :], AF.Square, py.res, hT[j].res)
            for j in range(2):
                tc = half * 2 + j
                self.rms_post_add(yv[j], y[j].res, hv[j], hT[j].res, self.gain(l, gpost), 0.5, tc)

    def wload(self, dst_t, dst_ap, src_ap):
        self.dma('pool', dst_ap, src_ap, [], dst_t.res)

    def xatt(self, l, s):
        TT = self.TT
        NCH = TT // 512
        wq_d = self.w['w_xq'][l].rearrange("(kc p) n -> p kc n", p=128)
        wkv_d = self.w['w_xkv'][l].rearrange("(kc p) n -> p kc n", p=128)
        wo_d = self.w['w_xo'][l].rearrange("(kc p) n -> p kc n", p=128)
        self.ar_reset()
        wq = self.al(DC * D * 2); wqv = wq[:, :].rearrange("p (c n) -> p c n", c=DC)
        wo = self.al(DC * D * 2); wov = wo[:, :].rearrange("p (c n) -> p c n", c=DC)
        self.wload(wq, wqv, wq_d)
        self.wload(wo, wov, wo_d)
        mnT = self.al(DC * 256 * 2); mnTv = mnT[:, :].rearrange("p (c t) -> p c t", c=DC)
        kxT = self.al(DC * 256 * 2); kxTv = kxT[:, :].rearrange("p (c t) -> p c t", c=DC)
        VW = 4 * 257
        vx = self.al(2 * VW * 2); vxv = vx[:, :].rearrange("p (m h e) -> p m h e", m=2, h=4)
        hT = self.al(DC * 512 * 2); hv = hT[:, :].rearrange("p (c t) -> p c t", c=DC)
        qT = self.al(DC * 512 * 2); qv = qT[:, :].rearrange("p (c t) -> p c t", c=DC)
        otok = self.al(4 * D * 2); otv = otok[:, :].rearrange("p (i n) -> p i n", i=4)
        oT = self.al(DC * 512 * 2); oTv = oT[:, :].rearrange("p (c t) -> p c t", c=DC)
        y = self.al(DC * 512 * 4, F32); yv = y[:, :].rearrange("p (c t) -> p c t", c=DC)
        wk = [self.al(DC * 128 * 2) for i in range(2)]
        wv_ = [self.al(DC * 512 * 2) for i in range(1)]
        xi = self.al(2 * D * 4, F32); xiv = xi[:, :].rearrange("p (m n) -> p m n", m=2)
        memT = y; memTv = y[:, 0:DC * 256].rearrange("p (c t) -> p c t", c=DC)
        self.alloc_small()
        self.dma('sp', xiv, self.mem_d[s].rearrange("(m p) n -> p m n", p=128), [], xi.res)
        for m in range(2):
            for half in range(2):
                bank = self.pb[self.rr('pbx', 2)]
                for c4 in range(4):
                    c = half * 4 + c4
                    self.tr(bank[:, c4 * 128:(c4 + 1) * 128], xiv[:, m, c * 128:(c + 1) * 128], self.ident_f[:, :],
                            xi.res + self.ident_f.res, bank.res)
                self.act(memTv[:, half * 4:half * 4 + 4, m * 128:(m + 1) * 128],
                         bank[:, :].rearrange("p (c t) -> p c t", c=4), AF.Copy, bank.res, memT.res)
        stat = self.pb[6]
        rstd = self.rstd[self.rr('rstd', 2)]
        self.act(mnTv, memTv, AF.Square, memT.res, mnT.res)
        for c in range(DC):
            self.mm(stat[:, 0:256], self.onesD, mnTv[:, c, :], c == 0, c == DC - 1, mnT.res + self.cst_bf.res, stat.res)
        self.v('dve', 'tensor_scalar', (rstd[:, 0:256], stat[:, 0:256], EPS, None, ALU.add), stat.res, rstd.res)
        self.act(rstd[:, 0:256], rstd[:, 0:256], AF.Sqrt, rstd.res, rstd.res)
        self.v('dve', 'reciprocal', (rstd[:, 0:256], rstd[:, 0:256]), rstd.res, rstd.res)
        g = self.gain(l, 8)
        for c in range(DC):
            self.v('dve', 'scalar_tensor_tensor', (mnTv[:, c, :], memTv[:, c, :], g[:, c:c + 1], rstd[:, 0:256], ALU.mult, ALU.mult),
                   memT.res + rstd.res + self.gains.res, mnT.res)
        for j in range(DC):
            wt = wk[self.rr('wk', 2)]
            wtv = wt[:, :].rearrange("p (c n) -> p c n", c=DC)
            self.wload(wt, wtv, wkv_d[:, :, j * 128:(j + 1) * 128])
            pa = self.pb[self.rr('pa', 2)]
            for c in range(DC):
                self.mm(pa[:, 0:256], wtv[:, c, :], mnTv[:, c, :], c == 0, c == DC - 1, wt.res + mnT.res, pa.res)
            self.act(kxTv[:, j, :], pa[:, 0:256], AF.Copy, pa.res, kxT.res)
        self.v('dve', 'memset', (vxv[:, :, :, 256:257], 1.0), [], vx.res)
        for n2 in range(2):
            wt = wv_[0]
            wtv = wt[:, :].rearrange("p (c n) -> p c n", c=DC)
            self.wload(wt, wtv, wkv_d[:, :, D + n2 * 512:D + (n2 + 1) * 512])
            for m in range(2):
                pa = self.pb[self.rr('pa', 2)]
                for c in range(DC):
                    self.mm(pa[:, :], mnTv[:, c, m * 128:(m + 1) * 128], wtv[:, c, :], c == 0, c == DC - 1,
                            wt.res + mnT.res, pa.res)
                self.act(vxv[:, m, 2 * n2:2 * n2 + 2, 0:256], pa[:, :].rearrange("p (h e) -> p h e", h=2), AF.Copy,
                         pa.res, vx.res)
        for tc in range(NCH):
            self.rms_pre(self.xT[:, :, tc * 512:(tc + 1) * 512], [self.xT.res[tc]], self.gain(l, 4),
                         hv, hT.res, hv, hT.res)
            for j in range(DC):
                py = self.pb[4 + self.rr('py', 2)]
                for c in range(DC):
                    self.mm(py[:, :], wqv[:, c, j * 128:(j + 1) * 128], hv[:, c, :], c == 0, c == DC - 1,
                            wq.res + hT.res, py.res)
                self.act(qv[:, j, :], py[:, :], AF.Copy, py.res, qT.res)
            for h in range(4):
                pts = []
                for m in range(2):
                    pa = self.pb[self.rr('pa', 2)]
                    for jj in range(2):
                        self.mm(pa[:, :], kxTv[:, 2 * h + jj, m * 128:(m + 1) * 128], qv[:, 2 * h + jj, :], jj == 0, jj == 1,
                                kxT.res + qT.res, pa.res)
                    pt = self.tmpb[self.rr('tmpb', 4)]
                    self.act(pt[:, :], pa[:, :], AF.Exp, pa.res, pt.res, scale=1.0 / 16.0)
                    pts.append(pt)
                for i in range(4):
                    po = self.pb[2 + self.rr('pb', 2)]
                    for m in range(2):
                        self.mm(po[:, 0:257], pts[m][:, i * 128:(i + 1) * 128], vxv[:, m, h, :], m == 0, m == 1,
                                pts[m].res + vx.res, po.res)
                    k = self.rr('sm', 8)
                    rd = self.sm[:, k:k + 1]
                    self.v('dve', 'reciprocal', (rd, po[:, 256:257]), po.res, self.sm.res)
                    self.v('dve', 'tensor_scalar', (otv[:, i, h * 256:(h + 1) * 256], po[:, 0:256], rd, None, ALU.mult),
                           po.res + self.sm.res, otok.res)
            for c in range(DC):
                pT = self.pbT
                for i in range(4):
                    self.tr(pT[:, i * 128:(i + 1) * 128], otv[:, i, c * 128:(c + 1) * 128], self.ident,
                            otok.res + self.cst_bf.res, pT.res)
                if c % 2 == 0:
                    self.act(oTv[:, c, :], pT[:, 0:512], AF.Copy, pT.res, oT.res)
                else:
                    self.v('dve', 'tensor_copy', (oTv[:, c, :], pT[:, 0:512]), pT.res, oT.res)
            for d in range(DC):
                py = self.pb[4 + self.rr('py', 2)]
                for c in range(DC):
                    self.mm(py[:, :], wov[:, c, d * 128:(d + 1) * 128], oTv[:, c, :], c == 0, c == DC - 1,
                            wo.res + oT.res, py.res)
                self.act(yv[:, d, :], py[:, :], AF.Copy, py.res, y.res)
                self.act(hv[:, d, :], py[:, :], AF.Square, py.res, hT.res)
            self.rms_post_add(yv, y.res, hv, hT.res, self.gain(l, 5), 1.0, tc)

    def mark(self):
        return self.ar_off

    def release(self, m, top=False):
        self.ar_off = m
        if top:
            self.ar_top = self.AR_BYTES

    def spv(self, l, off, n=1):
        o = l * SPW + off
        return self.spar[:, o:o + n]

    def wtile(self, W, cols, ring='wr', nring=3, width=128):
        tiles = self.rings[ring]
        wt = tiles[self.rr(ring, len(tiles))]
        wtv = wt[:, :].rearrange("p (c n) -> p c n", c=DC)
        for (o, c0, n) in cols:
            self.wload(wt, wtv[:, :, o:o + n], W[:, :, c0:c0 + n])
        return wt, wtv

    def proj_fm(self, wt, wtv, hv, hres, tc, bank, m0=0, m1=128, ncols=512):
        for c in range(DC):
            self.mm(bank[0:m1 - m0, 0:ncols], wtv[:, c, m0:m1], hv[:, c, tc * 512:tc * 512 + ncols], c == 0, c == DC - 1,
                    wt.res + hres, bank.res)

    def mixer(self, l, s):
        TT = self.TT
        NCH = TT // 512
        NT = TT // 128
        W = self.w['w_in'][l].rearrange("(kc p) n -> p kc n", p=128)
        self.ar_reset()
        hT = self.al(DC * TT * 2)
        hv = hT[:, :].rearrange("p (c t) -> p c t", c=DC)
        self.alloc_small()
        self.rings = {'wr': [self.al(DC * 128 * 2) for i in range(3)]}
        base = self.mark()
        for tc in range(NCH):
            sl = slice(tc * 512, (tc + 1) * 512)
            self.rms_pre(self.xT[:, :, sl], [self.xT.res[tc]], self.gain(l, 2), hv[:, :, sl], hT.res, hv[:, :, sl], hT.res)
        self.dma('sp', self.xs_d[:, :, :], self.xT[:, :, :], self.xT.res, [self.xs_res])
        ybf = self.xT.h[:, :, :].rearrange("p c t -> p (c t)").bitcast(BF)
        yv = [ybf[:, n * 4 * TT:(n + 1) * 4 * TT].rearrange("p (c t) -> p c t", c=4) for n in range(4)]
        yres = self.xT.res
        br = self.cfg.get('branches', 'abcd')
        if 'a' in br:
            self.mix_conv(l, W, hv, hT.res, yv[0], yres)
            self.release(base, True)
        if 'b' in br:
            self.mix_pool(l, W, hv, hT.res, yv[1], yres)
            self.release(base, True)
        if 'd' in br:
            self.mix_swa(l, s, W, hv, hT.res, yv[3], yres)
            self.release(base, True)
        if 'c' in br:
            self.mix_nsa(l, s, W, hv, hT.res, yv[2], yres)
            self.release(base, True)
        if self.cfg.get('debug'):
            self.dma('sp', self.dbg_d[:, :], ybf, yres, [])
        if self.cfg.get('merge', True):
            self.mix_merge(l, hT, hv, yv, yres)
        else:
            self.dma('sp', self.xT[:, :, :], self.xs_d[:, :, :], [self.xs_res], self.xT.res)

    def mix_conv(self, l, W, hv, hres, yout, yres):
        TT = self.TT
        NCH = TT // 512
        PAD = 32
        vts = [self.al((PAD + TT) * 2) for _ in range(2)]
        yc = [self.al(TT * 4, F32) for c in range(4)]
        ybf = self.al(4 * 512 * 2)
        ysq = self.al(4 * 512 * 2)
        dg = self.al(31 * 128 * 2)
        dgv = dg[:, :].rearrange("p (k n) -> p k n", k=31)
        ybv = ybf[:, :].rearrange("p (c t) -> p c t", c=4)
        ysv = ysq[:, :].rearrange("p (c t) -> p c t", c=4)
        self.tmpf2 = [self.al(2048, F32) for _ in range(2)]
        for vt in vts:
            self.v('dve', 'memset', (vt[:, 0:PAD], 0.0), [], vt.res)
        for c in range(4):
            vt = vts[c % 2]
            wa, wav = self.wtile(W, [(0, c * 128, 128)])
            wg, wgv = self.wtile(W, [(0, 512 + c * 128, 128)])
            for tc in range(NCH):
                pa = self.pb[self.rr('pa', 2)]
                pg = self.pb[2 + self.rr('pb', 2)]
                self.proj_fm(wa, wav, hv, hres, tc, pa)
                self.proj_fm(wg, wgv, hv, hres, tc, pg)
                sg = self.tmpf[self.rr('tmpf', 2)]
                self.act(sg[:, :], pg[:, :], AF.Sigmoid, pg.res, sg.res)
                self.v('dve', 'tensor_tensor', (vt[:, PAD + tc * 512:PAD + (tc + 1) * 512], pa[:, :], sg[:, :], ALU.mult),
                       pa.res + sg.res, vt.res)
            cw = self.spv(l, c * 31, 31)
            cb = self.spv(l, 124 + c)
            for k in range(31):
                self.v('dve', 'tensor_scalar', (dgv[:, k, :], self.ident, cw[:, k:k + 1], None, ALU.mult),
                       self.cst_bf.res + self.spar.res, dg.res)
            for tc in range(NCH):
                py = self.pb[4 + self.rr('py', 2)]
                for k in range(31):
                    o = PAD - 30 + k + tc * 512
                    self.mm(py[:, :], dgv[:, k, :], vt[:, o:o + 512], k == 0, k == 30, dg.res + vt.res, py.res)
                self.act(yc[c][:, tc * 512:(tc + 1) * 512], py[:, :], AF.Identity, py.res + self.spar.res, yc[c].res, bias=cb)
        for tc in range(NCH):
            sl = slice(tc * 512, (tc + 1) * 512)
            for c in range(4):
                self.act(ybv[:, c, :], yc[c][:, sl], AF.Copy, yc[c].res, ybf.res)
                self.act(ysv[:, c, :], yc[c][:, sl], AF.Square, yc[c].res, ysq.res)
            pm = self.pb[self.rr('pa', 2)]
            p2 = self.pb[2 + self.rr('pb', 2)]
            for c in range(4):
                self.mm(pm[:, :], self.onesC, ybv[:, c, :], c == 0, c == 3, ybf.res + self.cst_bf.res, pm.res)
            for c in range(4):
                self.mm(p2[:, :], self.onesC, ysv[:, c, :], c == 0, c == 3, ysq.res + self.cst_bf.res, p2.res)
            mean = self.tmpf[self.rr('tmpf', 2)]
            rstd = self.rstd[self.rr('rstd', 2)]
            self.act(mean[:, :], pm[:, :], AF.Copy, pm.res, mean.res)
            self.v('dve', 'tensor_tensor', (rstd[:, :], mean[:, :], mean[:, :], ALU.mult), mean.res, rstd.res)
            self.v('dve', 'tensor_tensor', (rstd[:, :], p2[:, :], rstd[:, :], ALU.subtract), p2.res + rstd.res, rstd.res)
            self.v('dve', 'tensor_scalar', (rstd[:, :], rstd[:, :], EPS, None, ALU.add), rstd.res, rstd.res)
            self.act(rstd[:, :], rstd[:, :], AF.Sqrt, rstd.res, rstd.res)
            self.v('dve', 'reciprocal', (rstd[:, :], rstd[:, :]), rstd.res, rstd.res)
            for c in range(4):
                t = self.tmpf2[self.rr('tmpf2', 2)]
                self.v('dve', 'tensor_tensor', (t[:, :], yc[c][:, sl], mean[:, :], ALU.subtract), yc[c].res + mean.res, t.res)
                self.v('dve', 'tensor_tensor', (t[:, :], t[:, :], rstd[:, :], ALU.mult), t.res + rstd.res, t.res)
                self.act(yout[:, c, sl], t[:, :], AF.Silu, t.res + self.spar.res, yres,
                         scale=self.spv(l, 128 + c), bias=self.spv(l, 132 + c))

    def mix_pool(self, l, W, hv, hres, yout, yres):
        TT = self.TT
        NCH = TT // 512
        PAD = 16
        up = self.al((PAD + TT) * 4, F32)
        A = self.al((PAD + TT) * 4, F32)
        B = self.al((PAD + TT) * 4, F32)
        rc = self.al(TT * 4, F32)
        dB = self.al(TT * 2)
        wp = self.al(4 * 128 * 2)
        wpv = wp[:, :].rearrange("p (g n) -> p g n", g=4)
        self.wload(wp, wpv, self.w['pool_w'][l].rearrange("g p n -> p g n"))
        for t_ in (up, A, B):
            self.v('dve', 'memset', (t_[:, 0:PAD], 0.0), [], t_.res)
        for c in range(4):
            win = 2 << c
            wt, wtv = self.wtile(W, [(0, 1024 + c * 128, 128)])
            for tc in range(NCH):
                pa = self.pb[self.rr('pa', 2)]
                self.proj_fm(wt, wtv, hv, hres, tc, pa)
                self.act(up[:, PAD + tc * 512:PAD + (tc + 1) * 512], pa[:, :], AF.Copy, pa.res, up.res)
            src = up
            sh = 1
            bufs = [A, B]
            bi = 0
            while sh < win:
                dst = bufs[bi]
                bi ^= 1
                self.v('dve', 'tensor_tensor', (dst[:, PAD:PAD + TT], src[:, PAD:PAD + TT], src[:, PAD - sh:PAD - sh + TT], ALU.add),
                       src.res, dst.res)
                src = dst
                sh *= 2
            self.v('dve', 'memset', (rc[:, :], 1.0 / win), [], rc.res)
            self.v('dve', 'tensor_copy', (rc[:, 0:16], self.gcst[:, 2 + c * 16:2 + (c + 1) * 16]), self.gcst.res + rc.res, rc.res)
            self.v('dve', 'tensor_tensor', (src[:, PAD:PAD + TT], src[:, PAD:PAD + TT], rc[:, :], ALU.mult), src.res + rc.res, src.res)
            self.v('dve', 'tensor_tensor', (dB[:, :], src[:, PAD:PAD + TT], up[:, PAD:PAD + TT], ALU.subtract), src.res + up.res, dB.res)
            for tc in range(NCH):
                pa = self.pb[self.rr('pa', 2)]
                self.mm(pa[:, :], wpv[:, c, :], dB[:, tc * 512:(tc + 1) * 512], True, True, wp.res + dB.res, pa.res)
                self.act(yout[:, c, tc * 512:(tc + 1) * 512], pa[:, :], AF.Copy, pa.res + self.spar.res, yres,
                         scale=self.spv(l, 136 + c))

    def rope_tables(self, s):
        TT = self.TT
        C = self.al_top(TT * 4, F32)
        Sg = self.al_top(TT * 4, F32)
        m = self.mark()
        posi = self.al(TT * 4, I32)
        u = self.al(TT * 4, F32)
        nf = self.al(TT * 4, F32)
        self.dma('sp', posi[:, :], self.pos_d[s].partition_broadcast(128), [], posi.res)
        posf = nf
        for which, dst in ((0, Sg), (1, C)):
            self.v('dve', 'tensor_copy', (posf[:, :], posi[:, :]), posi.res, posf.res)
            self.v('dve', 'tensor_scalar', (u[:, :], posf[:, :], self.gcst[:, 0:1], 0.25 * which, ALU.mult, ALU.add),
                   posf.res + self.gcst.res, u.res)
            nib = nf[:, :].bitcast(I32)
            self.v('dve', 'tensor_copy', (nib, u[:, :]), u.res, nf.res)
            self.v('dve', 'tensor_copy', (nf[:, :], nib), nf.res, nf.res)
            self.v('dve', 'tensor_tensor', (u[:, :], u[:, :], nf[:, :], ALU.subtract), u.res + nf.res, u.res)
            self.v('dve', 'tensor_scalar', (nf[:, :], u[:, :], 0.5, None, ALU.is_gt), u.res, nf.res)
            self.v('dve', 'tensor_tensor', (u[:, :], u[:, :], nf[:, :], ALU.subtract), u.res + nf.res, u.res)
            self.v('dve', 'tensor_scalar', (u[:, :], u[:, :], 0.4999999, -0.4999999, ALU.min, ALU.max), u.res, u.res)
            if which == 0:
                self.act(dst[:, :], u[:, :], AF.Sin, u.res + self.gcst.res, dst.res, scale=self.gcst[:, 1:2])
            else:
                self.act(dst[:, :], u[:, :], AF.Sin, u.res, dst.res, scale=2.0 * math.pi)
        self.release(m)
        return C, Sg

    def rope(self, pa, dst, dst_res, C, Sg, sl, P=128):
        qb = self.tmpb[self.rr('tmpb', 4)]
        self.act(qb[0:P, :], pa[0:P, :], AF.Copy, pa.res, qb.res)
        pr = self.pb[4 + self.rr('py', 2)]
        self.mm(pr[0:P, :], self.Rm[0:P, 0:P], qb[0:P, :], True, True, qb.res + self.cst_bf.res, pr.res)
        t1 = self.tmpf[self.rr('tmpf', 2)]
        t2 = self.tmpf[self.rr('tmpf', 2)]
        self.v('dve', 'tensor_tensor', (t1[0:P, :], pa[0:P, :], C[0:P, sl], ALU.mult), pa.res + C.res, t1.res)
        self.v('dve', 'tensor_tensor', (t2[0:P, :], pr[0:P, :], Sg[0:P, sl], ALU.mult), pr.res + Sg.res, t2.res)
        self.v('dve', 'tensor_tensor', (dst, t1[0:P, :], t2[0:P, :], ALU.add), t1.res + t2.res, dst_res)

    def tok_proj(self, wt, wtv, hv, hres, t, bank, ncols):
        for c in range(DC):
            self.mm(bank[:, 0:ncols], hv[:, c, t * 128:(t + 1) * 128], wtv[:, c, 0:ncols], c == 0, c == DC - 1,
                    wt.res + hres, bank.res)

    def attn_gen(self, qv, qres, kT, kres, i, g, blocks, scale, Wd, fin):
        r0 = g * 64
        po = self.pb[2 + self.rr('po', 4)]
        pov = po[:, :].rearrange("p (h e) -> p h e", h=4)
        MS = self.cfg.get('po_memset', False)
        if MS:
            self.v('dve', 'memset', (po[:, :], 0.0), [], po.res)
        nb = len(blocks)

        def emit_S(bi):
            j = blocks[bi][0]
            pa = self.pb[self.rr('pa', 2)]
            self.mm(pa[:, :].rearrange("p (h q) -> p h q", h=4), kT[r0:r0 + 64, j * 128:(j + 1) * 128],
                    qv[r0:r0 + 64, :, i * 128:(i + 1) * 128], True, True, kres + qres, pa.res)
            return pa
        pas = {0: emit_S(0)}
        yield
        for bi, (j, Vap, vres, masks) in enumerate(blocks):
            pa = pas.pop(bi)
            pt = self.tmpb[self.rr('tmpb', 4)]
            self.act(pt[:, :], pa[:, :], AF.Exp, pa.res, pt.res, scale=scale)
            if bi + 1 < nb:
                pas[bi + 1] = emit_S(bi + 1)
            ptv = pt[:, :].rearrange("p (h q) -> p h q", h=4)
            for mk in masks:
                m_ap, mres = mk() if callable(mk) else mk
                self.v('dve', 'tensor_tensor', (ptv, ptv, m_ap.unsqueeze(1).to_broadcast([128, 4, 128]), ALU.mult),
                       pt.res + mres, pt.res)
            for hh in range(4):
                self.mm(pov[:, hh, 0:Wd], pt[:, hh * 128:(hh + 1) * 128], Vap, (not MS) and bi == 0 and hh == 0, bi == nb - 1,
                        pt.res + vres, po.res, skip=True)
            yield
        fin(pov, po.res)

    def run_units(self, gens):
        gens = list(gens)
        assert len(gens) <= 2
        if len(gens) == 2 and self.cnt.get('pa', 0) % 2 == 1:
            self.cnt['pa'] += 1
        while gens:
            for gch in list(gens):
                try:
                    next(gch)
                except StopIteration:
                    gens.remove(gch)

    def attn_tile(self, *a):
        self.run_units([self.attn_gen(*a)])

    def o_to_y(self, o_b, yout, yres, i):
        pT = self.pbT
        for c in range(4):
            self.tr(pT[:, c * 128:(c + 1) * 128], o_b[:, c * 128:(c + 1) * 128], self.ident, o_b.res + self.cst_bf.res, pT.res)
        self.act(yout[:, :, i * 128:(i + 1) * 128], pT[:, 0:512].rearrange("p (c t) -> p c t", c=4), AF.Copy, pT.res, yres)

    def mix_swa(self, l, s, W, hv, hres, yout, yres):
        TT = self.TT
        NCH = TT // 512
        NT = TT // 128
        C, Sg = self.rope_tables(s)
        q = self.al(4 * TT * 2)
        qv = q[:, :].rearrange("p (c t) -> p c t", c=4)
        kt = self.al(TT * 2)
        vt = self.al(NT * 2 * 65 * 2)
        vv = vt[:, :].rearrange("p (t g e) -> p t g e", t=NT, g=2)
        ob = [self.al(512 * 2) for _ in range(2)]
        esink = self.al(8 * 4, F32)
        QS, KS, VS = 2840, 3352, 3480
        stop = self.cfg.get('swa_stop', 9)
        if stop < 1:
            return
        self.act(esink[:, :], self.spv(l, 145, 8), AF.Exp, self.spar.res, esink.res)
        if stop < 1.2:
            return
        self.v('dve', 'memset', (vv[:, :, :, 64:65], 1.0), [], vt.res)
        if stop < 1.4:
            return
        for i in range(4):
            wt, wtv = self.wtile(W, [(0, QS + i * 64, 64), (64, QS + (i + 4) * 64, 64)])
            for tc in range(NCH):
                pa = self.pb[self.rr('pa', 2)]
                self.proj_fm(wt, wtv, hv, hres, tc, pa)
                if stop < 1.6:
                    self.act(qv[:, i, tc * 512:(tc + 1) * 512], pa[:, :], AF.Copy, pa.res, q.res)
                    continue
                self.rope(pa, qv[:, i, tc * 512:(tc + 1) * 512], q.res, C, Sg, slice(tc * 512, (tc + 1) * 512))
        if stop < 2:
            return
        wt, wtv = self.wtile(W, [(0, KS, 128)])
        for tc in range(NCH):
            pa = self.pb[self.rr('pa', 2)]
            self.proj_fm(wt, wtv, hv, hres, tc, pa)
            self.rope(pa, kt[:, tc * 512:(tc + 1) * 512], kt.res, C, Sg, slice(tc * 512, (tc + 1) * 512))
        if stop < 3:
            return
        wt, wtv = self.wtile(W, [(0, VS, 128)])
        for t in range(NT):
            pa = self.pb[self.rr('pa', 2)]
            self.tok_proj(wt, wtv, hv, hres, t, pa, 128)
            self.act(vv[:, t, :, 0:64], pa[:, 0:128].rearrange("p (g e) -> p g e", g=2), AF.Copy, pa.res, vt.res)
        if stop < 4:
            return
        diag = (self.masks[:, 0:128], self.masks.res)
        edge = (self.masks[:, 128:256], self.masks.res)
        for i in range(NT):
            o_b = ob[self.rr('ob', 2)]
            units = []
            for g in range(2):
                blocks = []
                for j in (i - 1, i):
                    if j < 0:
                        continue
                    blocks.append((j, vv[:, j, g, :], vt.res, [diag] if j == i else [edge]))

                def fin(pov, pres, g=g, o_b=o_b):
                    k = self.rr('sm', 64) * 4
                    d4 = self.sm[:, k:k + 4].unsqueeze(2)
                    self.v('dve', 'tensor_tensor', (d4, pov[:, :, 64:65], esink[:, g * 4:(g + 1) * 4].unsqueeze(2), ALU.add),
                           pres + esink.res, self.sm.res)
                    self.v('dve', 'reciprocal', (d4, d4), self.sm.res, self.sm.res)
                    self.v('dve', 'tensor_tensor', (o_b[:, g * 256:(g + 1) * 256].rearrange("p (h e) -> p h e", h=4), pov[:, :, 0:64],
                                                    d4.to_broadcast([128, 4, 64]), ALU.mult), pres + self.sm.res, o_b.res)
                units.append(self.attn_gen(qv, q.res, kt, kt.res, i, g, blocks, 0.125, 65, fin))
            self.run_units(units)
            self.o_to_y(o_b, yout, yres, i)

    def mix_nsa(self, l, s, W, hv, hres, yout, yres):
        TT = self.TT
        NCH = TT // 512
        NT = TT // 128
        NC_ = (TT - 32) // 16 + 1
        kcmp = self.al(128 * 2)
        vcmp = self.al(2 * 97 * 2)
        vcv = vcmp[:, :].rearrange("p (g e) -> p g e", g=2)
        KV0 = 2048
        m1 = self.mark()
        C, Sg = self.rope_tables(s)
        kc = [self.al(TT * 2) for g in range(2)]
        vc = [self.al(TT * 2) for g in range(2)]
        W1 = self.al(32 * 256 * 2)
        W1v = W1[:, :].rearrange("p (l n) -> p l n", l=32)
        w2k = self.al(2 * 2 * 128 * 2)
        w2kv = w2k[:, :].rearrange("p (h g n) -> p h g n", h=2, g=2)
        w2v = self.al(2 * 64 * 2)
        w2vv = w2v[:, :].rearrange("p (h n) -> p h n", h=2)
        posT = self.al(32 * 2)
        hid = self.al(2 * 2 * 128 * 2)
        hidv = hid[:, :].rearrange("p (g h n) -> p g h n", g=2, h=2)
        xh = self.al(128 * 4, F32)
        uu = self.al(128 * 4, F32)
        b1t = self.al(2 * 4, F32)
        for g in range(2):
            wt, wtv = self.wtile(W, [(0, KV0 + g * 64, 64)])
            for tc in range(NCH):
                pa = self.pb[self.rr('pa', 2)]
                self.proj_fm(wt, wtv, hv, hres, tc, pa, 0, 64)
                self.rope(pa, kc[g][0:64, tc * 512:(tc + 1) * 512], kc[g].res, C, Sg, slice(tc * 512, (tc + 1) * 512), P=64)
            wt, wtv = self.wtile(W, [(0, KV0 + 128 + g * 64, 64)])
            for tc in range(NCH):
                pa = self.pb[self.rr('pa', 2)]
                self.proj_fm(wt, wtv, hv, hres, tc, pa, 0, 64)
                self.act(vc[g][0:64, tc * 512:(tc + 1) * 512], pa[0:64, :], AF.Copy, pa.res, vc[g].res)
        self.v('dve', 'memset', (kcmp[:, :], 0.0), [], kcmp.res)
        self.v('dve', 'memset', (vcmp[:, :], 0.0), [], vcmp.res)
        self.v('dve', 'memset', (w2k[:, :], 0.0), [], w2k.res)
        for kvi, (nm, src) in enumerate((('k', kc), ('v', vc))):
            self.wload(W1, W1v[0:64, :, :], self.w[f'cmp_{nm}_w1'][l].rearrange("(l p) n -> p l n", p=64))
            self.wload(posT, posT[0:64, :], self.w['cmp_posT'][l, kvi])
            w2_d = self.w[f'cmp_{nm}_w2'][l].rearrange("(h p) n -> p h n", p=128)
            if nm == 'k':
                for g in range(2):
                    self.wload(w2k, w2kv[:, :, g, g * 64:(g + 1) * 64], w2_d)
            else:
                self.wload(w2v, w2vv, w2_d)
            for hc in range(2):
                ps = self.pb[6]
                for ll in range(32):
                    self.mm(ps[:, 0:1], W1v[0:64, ll, hc * 128:(hc + 1) * 128], posT[0:64, ll:ll + 1], ll == 0, ll == 31,
                            W1.res + posT.res, ps.res)
                self.v('dve', 'tensor_tensor', (b1t[:, hc:hc + 1], ps[:, 0:1], self.spv(l, 140 + 2 * kvi + hc), ALU.add),
                       ps.res + self.spar.res, b1t.res)
            for g in range(2):
                for hc in range(2):
                    ps = self.pb[self.rr('pa', 2)]
                    for ll in range(32):
                        self.mm(ps[:, 0:NC_], W1v[0:64, ll, hc * 128:(hc + 1) * 128], src[g][0:64, ll:ll + 16 * (NC_ - 1) + 1:16],
                                ll == 0, ll == 31, W1.res + src[g].res, ps.res)
                    self.v('dve', 'tensor_scalar', (xh[:, 0:NC_], ps[:, 0:NC_], b1t[:, hc:hc + 1], None, ALU.add), ps.res + b1t.res, xh.res)
                    self.v('dve', 'tensor_tensor', (uu[:, 0:NC_], xh[:, 0:NC_], xh[:, 0:NC_], ALU.mult), xh.res, uu.res)
                    self.v('dve', 'tensor_scalar', (uu[:, 0:NC_], uu[:, 0:NC_], 0.044715, 1.0, ALU.mult, ALU.add), uu.res, uu.res)
                    self.v('dve', 'tensor_tensor', (uu[:, 0:NC_], uu[:, 0:NC_], xh[:, 0:NC_], ALU.mult), uu.res + xh.res, uu.res)
                    self.act(uu[:, 0:NC_], uu[:, 0:NC_], AF.Sigmoid, uu.res, uu.res, scale=2.0 * math.sqrt(2.0 / math.pi))
                    self.v('dve', 'tensor_tensor', (hidv[:, g, hc, 0:NC_], uu[:, 0:NC_], xh[:, 0:NC_], ALU.mult), uu.res + xh.res, hid.res)
            if nm == 'k':
                ps = self.pb[self.rr('pa', 2)]
                n = 0
                for g in range(2):
                    for hc in range(2):
                        self.mm(ps[:, 0:NC_], w2kv[:, hc, g, :], hidv[:, g, hc, 0:NC_], n == 0, n == 3, w2k.res + hid.res, ps.res)
                        n += 1
                self.v('dve', 'tensor_scalar', (kcmp[:, 0:NC_], ps[:, 0:NC_], self.spv(l, 144), None, ALU.add),
                       ps.res + self.spar.res, kcmp.res)
            else:
                for g in range(2):
                    ps = self.pb[self.rr('pa', 2)]
                    for hc in range(2):
                        self.mm(ps[0:NC_, 0:64], hidv[:, g, hc, 0:NC_], w2vv[:, hc, :], hc == 0, hc == 1, w2v.res + hid.res, ps.res)
                    self.v('dve', 'tensor_tensor', (vcv[0:NC_, g, 0:64], ps[0:NC_, 0:64], self.spar[0:NC_, l * SPW + 153:l * SPW + 217], ALU.add),
                           ps.res + self.spar.res, vcmp.res)
                    self.v('dve', 'memset', (vcv[:, g, 64:65], 1.0), vcmp.res, vcmp.res)
                    self.v('dve', 'tensor_copy', (vcv[:, g, 65:97], self.ovl[:, :]), vcmp.res + self.ovl.res, vcmp.res)
        self.release(m1, True)
        if self.cfg.get('nsa_stop', 9) < 2:
            return
        C, Sg = self.rope_tables(s)
        q = self.al(4 * TT * 2)
        qv = q[:, :].rearrange("p (c t) -> p c t", c=4)
        ks = self.al(TT * 2)
        kw = self.al(TT * 2)
        vs = self.al(NT * 2 * 65 * 2)
        vw = self.al(NT * 2 * 65 * 2)
        vsv = vs[:, :].rearrange("p (t g e) -> p t g e", t=NT, g=2)
        vwv = vw[:, :].rearrange("p (t g e) -> p t g e", t=NT, g=2)
        gate = self.al(NT * 24 * 4, F32)
        gv = gate[:, :].rearrange("p (t n) -> p t n", t=NT)
        m2 = self.mark()
        wtok = self.al(DC * 280 * 2)
        QS = 1536
        for i in range(4):
            wt, wtv = self.wtile(W, [(0, QS + i * 64, 64), (64, QS + (i + 4) * 64, 64)])
            for tc in range(NCH):
                pa = self.pb[self.rr('pa', 2)]
                self.proj_fm(wt, wtv, hv, hres, tc, pa)
                self.rope(pa, qv[:, i, tc * 512:(tc + 1) * 512], q.res, C, Sg, slice(tc * 512, (tc + 1) * 512))
        for (dst, c0) in ((ks, KV0 + 256), (kw, KV0 + 512)):
            wt, wtv = self.wtile(W, [(0, c0, 128)])
            for tc in range(NCH):
                pa = self.pb[self.rr('pa', 2)]
                self.proj_fm(wt, wtv, hv, hres, tc, pa)
                self.rope(pa, dst[:, tc * 512:(tc + 1) * 512], dst.res, C, Sg, slice(tc * 512, (tc + 1) * 512))
        wtv = wtok[:, :].rearrange("p (c n) -> p c n", c=DC)
        self.wload(wtok, wtv[:, :, 0:128], W[:, :, KV0 + 256 + 128:KV0 + 256 + 256])
        self.wload(wtok, wtv[:, :, 128:256], W[:, :, KV0 + 512 + 128:KV0 + 512 + 256])
        self.wload(wtok, wtv[:, :, 256:280], W[:, :, 2816:2840])
        self.v('dve', 'memset', (vsv[:, :, :, 64:65], 1.0), [], vs.res)
        self.v('dve', 'memset', (vwv[:, :, :, 64:65], 1.0), [], vw.res)
        for t in range(NT):
            pa = self.pb[self.rr('pa', 2)]
            self.tok_proj(wtok, wtv, hv, hres, t, pa, 280)
            self.act(vsv[:, t, :, 0:64], pa[:, 0:128].rearrange("p (g e) -> p g e", g=2), AF.Copy, pa.res, vs.res)
            self.act(vwv[:, t, :, 0:64], pa[:, 128:256].rearrange("p (g e) -> p g e", g=2), AF.Copy, pa.res, vw.res)
            self.act(gv[:, t, :], pa[:, 256:280], AF.Sigmoid, pa.res, gate.res)
        self.release(m2, True)
        if self.cfg.get('nsa_stop', 9) < 3:
            return
        Eg = self.al(2 * TT * 2)
        Egv = Eg[:, :].rearrange("p (g k) -> p g k", g=2)
        ob = [self.al(512 * 2) for _ in range(2)]
        of = [self.al(512 * 4, F32) for _ in range(2)]
        misc = self.al(2048, F32)
        ftmp = [self.al(2048, F32) for _ in range(2)]

        class _V:
            def __init__(s_, ap, res):
                s_.ap = ap
                s_.res = res

            def __getitem__(s_, k):
                return s_.ap[k]
        imp = _V(misc[:, 0:64], misc.res)
        sc1 = _V(misc[:, 64:128], misc.res)
        sc2 = _V(misc[:, 128:192], misc.res)
        m8 = _V(misc[:, 192:208], misc.res)
        selb = _V(misc[:, 256:288].bitcast(BF), misc.res)
        selT = [self.al(128 * 2) for _ in range(2)]
        self.dma('sp', Eg[0:64, :], self.Eg_d[:, :], [], Eg.res)
        cmask = self.al(16 * 128 * 2)
        tkb = self.al(NT * 64 * 4, F32)
        self.dma('sp', cmask[:, :], self.masks_d[:, 256:18 * 128], [], cmask.res)
        self.dma('sp', tkb[:, :], self.tkb_d[:, :], [], tkb.res)
        diag = (self.masks[:, 0:128], self.masks.res)
        edge = (self.masks[:, 128:256], self.masks.res)
        for i in range(NT):
            o_f = of[self.rr('of', 2)]
            o_b = ob[self.rr('ob', 2)]

            def fin_branch(pov, pres, g, bidx, first, want_imp=False, o_f=o_f, i=i):
                k = self.rr('sm', 32) * 8
                d4 = self.sm[:, k:k + 4].unsqueeze(2)
                s4 = self.sm[:, k + 4:k + 8].unsqueeze(2)
                if want_imp:
                    self.v('dve', 'tensor_scalar', (d4, pov[:, :, 64:65], 1e-30, None, ALU.max), pres, self.sm.res)
                    self.v('dve', 'reciprocal', (d4, d4), self.sm.res, self.sm.res)
                else:
                    self.v('dve', 'reciprocal', (d4, pov[:, :, 64:65]), pres, self.sm.res)
                gsl = gv[:, i, 12 * g + bidx:12 * g + 12:3].unsqueeze(2)
                self.v('dve', 'tensor_tensor', (s4, d4, gsl, ALU.mult), self.sm.res + gate.res, self.sm.res)
                osl = o_f[:, g * 256:(g + 1) * 256].rearrange("p (h e) -> p h e", h=4)
                if first:
                    self.v('dve', 'tensor_tensor', (osl, pov[:, :, 0:64], s4.to_broadcast([128, 4, 64]), ALU.mult),
                           pres + self.sm.res, o_f.res)
                else:
                    tmp = ftmp[self.rr('ftmp', 2)]
                    tv = tmp[:, 0:256].rearrange("p (h e) -> p h e", h=4)
                    self.v('dve', 'tensor_tensor', (tv, pov[:, :, 0:64], s4.to_broadcast([128, 4, 64]), ALU.mult),
                           pres + self.sm.res, tmp.res)
                    self.v('dve', 'tensor_tensor', (osl, osl, tv, ALU.add), tmp.res + o_f.res, o_f.res)
                if want_imp:
                    tmp = ftmp[self.rr('ftmp', 2)]
                    tv = tmp[:, 0:128].rearrange("p (h j) -> p h j", h=4)
                    self.v('dve', 'tensor_tensor', (tv, pov[:, :, 65:97], d4.to_broadcast([128, 4, 32]), ALU.mult),
                           pres + self.sm.res, tmp.res)
                    self.v('dve', 'tensor_reduce', (imp[:, g * 32:(g + 1) * 32], tmp[:, 0:128].rearrange("p (h j) -> p j h", h=4), AX.X, ALU.add),
                           tmp.res, imp.res)
            cm = (cmask[:, i * 128:(i + 1) * 128], cmask.res)
            self.run_units([self.attn_gen(qv, q.res, kcmp, kcmp.res, i, g, [(0, vcv[:, g, :], vcmp.res, [cm])], 0.125, 97,
                                          lambda pov, pres, g=g: fin_branch(pov, pres, g, 0, True, True)) for g in range(2)])
            dyn = (2 * i + 2) > 16
            st = None
            if dyn:
                st = selT[self.rr('selT', 2)]
                self.v('dve', 'tensor_tensor', (sc1[:, :], imp[:, :], tkb[:, i * 64:(i + 1) * 64], ALU.add), imp.res + tkb.res, sc1.res)
                for g in range(2):
                    gs = slice(g * 32, (g + 1) * 32)
                    self.v('dve', 'max', (), sc1.res, m8.res, out=m8[:, 0:8], in_=sc1[:, gs])
                    self.v('dve', 'match_replace', (), m8.res + sc1.res, sc2.res, out=sc2[:, gs], in_to_replace=m8[:, 0:8],
                           in_values=sc1[:, gs], imm_value=-3.0e38)
                    self.v('dve', 'max', (), sc2.res, m8.res, out=m8[:, 8:16], in_=sc2[:, gs])
                    self.v('dve', 'tensor_reduce', (m8[:, 0:1], m8[:, 8:16], AX.X, ALU.min), m8.res, m8.res)
                    self.v('dve', 'tensor_scalar', (selb[:, gs], sc1[:, gs], m8[:, 0:1], None, ALU.is_ge), sc1.res + m8.res, selb.res)
                pT = self.pbT
                self.tr(pT[0:64, 0:128], selb[:, 0:64], self.ident, selb.res + self.cst_bf.res, pT.res)
                self.act(st[0:64, :], pT[0:64, 0:128], AF.Copy, pT.res, st.res)
                if self.cfg.get('debug'):
                    self.dma('sp', self.dbg2_d[:, i * 128:(i + 1) * 128], st[0:64, :], st.res, [])
                    self.dma('sp', self.dbg3_d[:, i * 64:(i + 1) * 64], sc1[:, :], sc1.res, [])
            units = []
            for g in range(2):
                blocks = []
                for j in range(i + 1):
                    masks = []
                    if dyn:
                        def mkmask(g=g, j=j, st=st):
                            pr = self.rr('pm', 2)
                            pm = self.pb[6]
                            self.mm(pm[:, pr * 128:(pr + 1) * 128], Egv[0:64, g, j * 128:(j + 1) * 128], st[0:64, :], True, True,
                                    Eg.res + st.res, pm.res)
                            return (pm[:, pr * 128:(pr + 1) * 128], pm.res)
                        masks.append(mkmask)
                    if j == i:
                        masks.append(diag)
                    blocks.append((j, vsv[:, j, g, :], vs.res, masks))
                units.append(self.attn_gen(qv, q.res, ks, ks.res, i, g, blocks, 0.125, 65, lambda pov, pres, g=g: fin_branch(pov, pres, g, 1, False)))
            self.run_units(units)
            units = []
            for g in range(2):
                blocks = []
                for j in range(max(0, i - 4), i + 1):
                    masks = [diag] if j == i else ([edge] if j == i - 4 else [])
                    blocks.append((j, vwv[:, j, g, :], vw.res, masks))
                units.append(self.attn_gen(qv, q.res, kw, kw.res, i, g, blocks, 0.125, 65, lambda pov, pres, g=g: fin_branch(pov, pres, g, 2, False)))
            self.run_units(units)
            self.act(o_b[:, :], o_f[:, :], AF.Copy, o_f.res, o_b.res)
            self.o_to_y(o_b, yout, yres, i)

    def mix_merge(self, l, hT, hv, yv, yres):
        TT = self.TT
        NCH = TT // 512
        W = self.w['w_in'][l].rearrange("(kc p) n -> p kc n", p=128)
        Wb = self.w['w_branch'][l].rearrange("n (kc p) d -> p (n kc) d", p=128)
        Wo = self.w['w_out'][l].rearrange("(kc p) n -> p kc n", p=128)
        merged = self.al(DC * TT * 2)
        mv = merged[:, :].rearrange("p (c t) -> p c t", c=DC)
        m0 = self.mark()
        wg = [self.al(DC * 512 * 2) for _ in range(2)]
        wb = [self.al(16 * 128 * 2) for _ in range(2)]
        tf = [self.al(2048, F32) for _ in range(4)]
        GS = 3608
        for d in range(DC):
            wgt = wg[d % 2]
            wgv = wgt[:, :].rearrange("p (c n) -> p c n", c=DC)
            for n in range(4):
                self.wload(wgt, wgv[:, :, n * 128:(n + 1) * 128], W[:, :, GS + n * 1024 + d * 128:GS + n * 1024 + (d + 1) * 128])
            wbt = wb[d % 2]
            wbv = wbt[:, :].rearrange("p (k n) -> p k n", k=16)
            self.wload(wbt, wbv, Wb[:, :, d * 128:(d + 1) * 128])
            for tc in range(NCH):
                sl = slice(tc * 512, (tc + 1) * 512)
                acc = tf[self.rr('tfa', 2)]
                for n in range(4):
                    pg = self.pb[self.rr('pa', 2)]
                    pk = self.pb[2 + self.rr('pb', 2)]
                    for c in range(DC):
                        self.mm(pg[:, :], wgv[:, c, n * 128:(n + 1) * 128], hv[:, c, sl], c == 0, c == DC - 1, wgt.res + hT.res, pg.res)
                    for kc in range(4):
                        self.mm(pk[:, :], wbv[:, n * 4 + kc, :], yv[n][:, kc, sl], kc == 0, kc == 3, wbt.res + yres, pk.res)
                    sg = tf[2 + self.rr('tfs', 2)]
                    self.act(sg[:, :], pg[:, :], AF.Sigmoid, pg.res, sg.res)
                    if n == 0:
                        self.v('dve', 'tensor_tensor', (acc[:, :], sg[:, :], pk[:, :], ALU.mult), sg.res + pk.res, acc.res)
                    else:
                        self.v('dve', 'tensor_tensor', (sg[:, :], sg[:, :], pk[:, :], ALU.mult), sg.res + pk.res, sg.res)
                        dst = mv[:, d, sl] if n == 3 else acc[:, :]
                        dres = merged.res if n == 3 else acc.res
                        self.v('dve', 'tensor_tensor', (dst, acc[:, :], sg[:, :], ALU.add), acc.res + sg.res, dres)
        self.release(m0)
        self.dma('sp', self.xT[:, :, :], self.xs_d[:, :, :], [self.xs_res], self.xT.res)
        wo = self.al(DC * D * 2)
        wov = wo[:, :].rearrange("p (c n) -> p c n", c=DC)
        self.wload(wo, wov, Wo)
        yv_ = hT[:, 0:DC * 512 * 2].bitcast(F32).rearrange("p (c t) -> p c t", c=DC)
        sqv = hT[:, DC * 512 * 2:DC * 512 * 3].rearrange("p (c t) -> p c t", c=DC)
        for tc in range(NCH):
            sl = slice(tc * 512, (tc + 1) * 512)
            for d in range(DC):
                py = self.pb[4 + self.rr('py', 2)]
                for c in range(DC):
                    self.mm(py[:, :], wov[:, c, d * 128:(d + 1) * 128], mv[:, c, sl], c == 0, c == DC - 1, wo.res + merged.res, py.res)
                self.act(yv_[:, d, :], py[:, :], AF.Copy, py.res, hT.res)
                self.act(sqv[:, d, :], py[:, :], AF.Square, py.res, hT.res)
            self.rms_post_add(yv_, hT.res, sqv, hT.res, self.gain(l, 3), 1.0, tc)

    def alloc_small(self):
        self.rstd = [self.al(2048, F32) for i in range(2)]
        self.tmpf = [self.al(2048, F32) for i in range(2)]
        self.tmpb = [self.al(1024) for i in range(4)]
        self.sm = self.al(2048, F32)


GAIN_NAMES = ['ffn1_pre_g', 'ffn1_post_g', 'mix_pre_g', 'mix_post_g', 'x_pre_g', 'x_post_g', 'ffn2_pre_g', 'ffn2_post_g', 'mem_g']


def host_consts(inputs, L, TT=2048):
    bf = ml_dtypes.bfloat16
    NT = TT // 128
    p = np.arange(128)
    cst = np.zeros((128, 512), np.float32)
    cst[:, 0:128] = np.eye(128, dtype=np.float32)
    cst[:, 128:256] = 1.0 / D
    cst[:, 256:384] = 1.0 / 512
    partner = np.where((p % 64) < 32, p + 32, p - 32)
    Rm = np.zeros((128, 128), np.float32)
    Rm[partner, p] = 1.0
    cst[:, 384:512] = Rm
    gains = np.concatenate(
        [np.asarray(inputs[n][l], np.float32).reshape(DC, 128).T for l in range(L) for n in GAIN_NAMES], axis=1)
    kl = p[:, None]
    ql = p[None, :]
    masks = [(kl <= ql), (ql < kl)]
    for i in range(16):
        masks.append((16 * kl + 31 <= 128 * i + ql) & (kl < 127))
    masks = np.concatenate([m.astype(np.float32) for m in masks], axis=1)
    tkb = np.zeros((128, NT, 2, 32), np.float32)
    for i in range(NT):
        qpos = 128 * i + p
        j = np.arange(32)
        causal = (64 * j[None, :] <= qpos[:, None])
        forced = (j[None, :] == 0) | (j[None, :] == (qpos // 64)[:, None])
        val = np.where(causal, np.where(forced, 1e4, 0.0), -1e30).astype(np.float32)
        tkb[:, i, 0, :] = val
        tkb[:, i, 1, :] = val
    gc = np.zeros((128, 66), np.float32)
    inv = 10000.0 ** (-(2.0 * (p % 32)) / 64.0)
    gc[:, 0] = inv / (2 * np.pi)
    gc[:, 1] = 2 * np.pi * np.where((p % 64) < 32, -1.0, 1.0)
    for c in range(4):
        gc[:, 2 + c * 16:2 + (c + 1) * 16] = 1.0 / np.minimum(np.arange(16) + 1, 2 << c)
    cs = np.arange(128) * 16
    js = np.arange(32) * 64
    ovl = ((cs[:, None] <= js[None, :] + 63) & (cs[:, None] + 31 >= js[None, :]) & (np.arange(128)[:, None] < 127))
    Eg = np.zeros((64, 2, TT), np.float32)
    keys = np.arange(TT)
    for g in range(2):
        Eg[32 * g + keys // 64, g, keys] = 1.0
    spar = []
    for l in range(L):
        f = lambda n: np.asarray(inputs[n][l], np.float32)
        cols = [f('conv_w').T.reshape(4, 128, 31).transpose(1, 0, 2).reshape(128, 124)]
        for n in ('conv_b', 'conv_ln_g', 'conv_ln_b', 'pool_scale'):
            cols.append(f(n).reshape(4, 128).T)
        cols.append(f('cmp_k_b1').reshape(2, 128).T)
        cols.append(f('cmp_v_b1').reshape(2, 128).T)
        cols.append(np.tile(f('cmp_k_b2'), 2)[:, None])
        cols.append(np.broadcast_to(f('swa_sinks')[None, :], (128, 8)))
        cols.append(np.broadcast_to(f('cmp_v_b2')[None, :], (128, 64)))
        spar.append(np.concatenate(cols, axis=1))
    spar = np.concatenate(spar, axis=1)
    assert spar.shape[1] == L * SPW
    posT = np.stack([np.stack([np.asarray(inputs['cmp_k_pos'][l], np.float32).T,
                               np.asarray(inputs['cmp_v_pos'][l], np.float32).T]) for l in range(L)])
    return {
        'cst_bf': cst.astype(bf),
        'ident_f': np.eye(128, dtype=np.float32),
        'gains': np.ascontiguousarray(gains),
        'masks': masks.astype(bf),
        'tkb': np.ascontiguousarray(tkb.reshape(128, NT * 64)),
        'gcst': gc,
        'spar': np.ascontiguousarray(spar),
        'ovl': ovl.astype(np.float32).astype(bf),
        'Eg': Eg.reshape(64, 2 * TT).astype(bf),
        'cmp_posT': np.ascontiguousarray(posT),
    }


_CACHE = {}


WNAMES = ['ffn1_w_in', 'ffn1_w_out', 'ffn2_w_in', 'ffn2_w_out', 'w_xq', 'w_xkv', 'w_xo', 'w_in', 'pool_w',
          'cmp_k_w1', 'cmp_v_w1', 'cmp_k_w2', 'cmp_v_w2', 'w_branch', 'w_out']


def kernel(**inputs):
    NCORE = 8
    B = inputs['x'].shape[0]
    L = inputs['ffn1_w_in'].shape[0]
    nseq = B // NCORE
    cfg = {'nseq': nseq, 'layers': L, 'T': inputs['x'].shape[1]}
    b = Builder(cfg)
    nc = b.build()
    consts = host_consts(inputs, L)
    in_maps = []
    for c in range(NCORE):
        m = dict(consts)
        m['x'] = np.ascontiguousarray(inputs['x'][c * nseq:(c + 1) * nseq])
        m['mem'] = np.ascontiguousarray(inputs['mem'][c * nseq:(c + 1) * nseq])
        m['positions'] = np.ascontiguousarray(inputs['positions'][c * nseq:(c + 1) * nseq]).astype(np.int32)
        for k in WNAMES:
            m[k] = np.asarray(inputs[k])
        in_maps.append(m)
    res = run_bass_kernel_spmd(nc, in_maps, core_ids=list(range(NCORE)))
    return np.concatenate([r['out'] for r in res.results], axis=0)
```
